# Optimizing a Trainium2 kernel written in Bass

```python
import jax, jax.numpy as jnp
from jax import lax
import numpy as np

D_MODEL = 2048
BATCH = 16
SEQ = 2048
DEPTH = 1
DEC_BATCH = 16
DEC_SEQ = 16
PAST_LEN = 2048

CHUNK = 64
H_A = 16
DK_A = 128
DV_A = 128
CONV_W = 4
GDN_QK = H_A * DK_A
GDN_V = H_A * DV_A
GDN_CONV_DIM = 2 * GDN_QK + GDN_V
H_B = 8
DQK_B = 128
DV_B = 256
M_QK = H_B * DQK_B
M_V = H_B * DV_B
PEER_HEADS = 8
N_KEYS = 128
N_EXPERTS = N_KEYS * N_KEYS
PEER_DKEY = 256
PEER_TOPK = 16
PEER_BLOCK = 128
PLE_DIM = 256
ALPHA = (2 * DEPTH) ** 0.25
BETA = (8 * DEPTH) ** -0.25
LN_EPS = 1e-5
RMS_EPS = 1e-6
IN_SPLITS = (GDN_CONV_DIM, GDN_V, H_A, H_A, M_QK, M_QK, M_V, M_V, H_B, H_B, D_MODEL, D_MODEL)
N_IN = sum(IN_SPLITS)

kernel_name = "hybrid_gdn_mlstm_peer_stream_step"


def _layer_norm(x, g, b):
    xf = x.astype(jnp.float32)
    mu = jnp.mean(xf, -1, keepdims=True)
    var = jnp.mean(jnp.square(xf - mu), -1, keepdims=True)
    return ((xf - mu) * lax.rsqrt(var + LN_EPS) * g + b).astype(x.dtype)


def _rms_norm(x, w):
    return x * lax.rsqrt(jnp.mean(x * x, -1, keepdims=True) + RMS_EPS) * w


def _l2norm(x):
    return x * lax.rsqrt(jnp.sum(x * x, -1, keepdims=True) + 1e-6)


def _chunk_len(L):
    return CHUNK if L % CHUNK == 0 else L


def _to_chunks(x, c):
    B, L, H = x.shape[:3]
    rest = x.shape[3:]
    x = x.reshape((B, L // c, c, H) + rest)
    return x.transpose((1, 0, 3, 2) + tuple(range(4, x.ndim)))


def _from_chunks(x):
    n, B, H, c, D = x.shape
    return x.transpose(1, 0, 3, 2, 4).reshape(B, n * c, H, D)


def _gated_delta_rule(q, k, v, g, beta, s0):
    c = _chunk_len(q.shape[1])
    q, k, v = _to_chunks(q, c), _to_chunks(k, c), _to_chunks(v, c)
    g, beta = _to_chunks(g, c), _to_chunks(beta, c)
    G = jnp.cumsum(g, -1)
    incl = jnp.tril(jnp.ones((c, c), bool))
    strict = jnp.tril(jnp.ones((c, c), bool), -1)
    decay = jnp.exp(jnp.where(incl, G[..., :, None] - G[..., None, :], -jnp.inf))
    kb = k * beta[..., None]
    a_mat = jnp.einsum('nbhid,nbhjd->nbhij', kb, k) * decay
    a_mat = jnp.where(strict, a_mat, 0.0) + jnp.eye(c, dtype=a_mat.dtype)
    rhs = jnp.concatenate([v * beta[..., None], kb * jnp.exp(G)[..., None]], -1)
    sol = lax.linalg.triangular_solve(a_mat, rhs, left_side=True, lower=True, unit_diagonal=True)
    dv = v.shape[-1]
    u_pre, w = sol[..., :dv], sol[..., dv:]
    attn = jnp.einsum('nbhid,nbhjd->nbhij', q, k) * decay
    q_g = q * jnp.exp(G)[..., None]
    k_dec = k * jnp.exp(G[..., -1:] - G)[..., None]
    g_end = jnp.exp(G[..., -1])

    def step(s, inp):
        u_pre_i, w_i, attn_i, q_i, k_i, ge = inp
        u = u_pre_i - jnp.einsum('bhck,bhkv->bhcv', w_i, s)
        o = jnp.einsum('bhck,bhkv->bhcv', q_i, s) + jnp.einsum('bhij,bhjv->bhiv', attn_i, u)
        s = s * ge[..., None, None] + jnp.einsum('bhck,bhcv->bhkv', k_i, u)
        return s, o

    s_end, o = lax.scan(step, s0, (u_pre, w, attn, q_g, k_dec, g_end))
    return _from_chunks(o), s_end


def _mlstm(q, k, v, ig, lf, c0, n0, m0):
    c = _chunk_len(q.shape[1])
    q, k, v = _to_chunks(q, c), _to_chunks(k, c), _to_chunks(v, c)
    ig, lf = _to_chunks(ig, c), _to_chunks(lf, c)
    F = jnp.cumsum(lf, -1)
    incl = jnp.tril(jnp.ones((c, c), bool))
    d_log = jnp.where(incl, F[..., :, None] - F[..., None, :] + ig[..., None, :], -jnp.inf)
    i_max = jnp.max(d_log, -1)
    w_intra = jnp.exp(d_log - i_max[..., None]) * jnp.einsum('nbhid,nbhjd->nbhij', q, k)
    num_intra = jnp.einsum('nbhij,nbhjv->nbhiv', w_intra, v)
    den_intra = jnp.sum(w_intra, -1)
    e_max = i_max[..., -1]
    k_end = k * jnp.exp(d_log[..., -1, :] - e_max[..., None])[..., None]

    def step(carry, inp):
        cm, nv, m = carry
        q_i, v_i, k_i, F_i, imax_i, numi, deni, emax_i = inp
        m_t = jnp.maximum(m[..., None] + F_i, imax_i)
        inter = jnp.exp(m[..., None] + F_i - m_t)
        intra = jnp.exp(imax_i - m_t)
        num = inter[..., None] * jnp.einsum('bhck,bhkv->bhcv', q_i, cm) + intra[..., None] * numi
        den = inter * jnp.einsum('bhck,bhk->bhc', q_i, nv) + intra * deni
        h = num / jnp.maximum(jnp.abs(den), jnp.exp(-m_t))[..., None]
        m_new = m_t[..., -1]
        a = jnp.exp(m + F_i[..., -1] - m_new)
        b = jnp.exp(emax_i - m_new)
        cm = a[..., None, None] * cm + b[..., None, None] * jnp.einsum('bhck,bhcv->bhkv', k_i, v_i)
        nv = a[..., None] * nv + b[..., None] * jnp.sum(k_i, -2)
        return (cm, nv, m_new), h

    (c_end, n_end, m_end), h = lax.scan(step, (c0, n0, m0),
                                        (q, v, k_end, F, i_max, num_intra, den_intra, e_max))
    return _from_chunks(h), c_end, n_end, m_end


def _token_mixers(x, conv_buf, s_gdn, c_m, n_m, m_m, w_in, conv_w, a_log, dt_bias, gdn_norm_w,
                  b_i, b_f, m_norm_w, w_br_a, w_br_b, w_out):
    B, L, _ = x.shape
    f32 = jnp.float32
    proj = x @ w_in
    cuts = [int(s) for s in np.cumsum(IN_SPLITS)[:-1]]
    qkv, z, b_raw, a_raw, qm, km, vm, om, im, fm, ga, gb = jnp.split(proj, cuts, axis=-1)
    xpad = jnp.concatenate([conv_buf.astype(qkv.dtype), qkv], 1)
    conv = xpad[:, 0:L] * conv_w[0]
    for w in range(1, CONV_W):
        conv = conv + xpad[:, w:w + L] * conv_w[w]
    new_buf = xpad[:, xpad.shape[1] - (CONV_W - 1):]
    conv = jax.nn.silu(conv.astype(f32))
    qa, ka, va = jnp.split(conv, [GDN_QK, 2 * GDN_QK], axis=-1)
    qa = _l2norm(qa.reshape(B, L, H_A, DK_A)) * (DK_A ** -0.5)
    ka = _l2norm(ka.reshape(B, L, H_A, DK_A))
    va = va.reshape(B, L, H_A, DV_A)
    beta = jax.nn.sigmoid(b_raw.astype(f32))
    g = -jnp.exp(a_log.astype(f32)) * jax.nn.softplus(a_raw.astype(f32) + dt_bias.astype(f32))
    oa, s_new = _gated_delta_rule(qa, ka, va, g, beta, s_gdn.astype(f32))
    oa = _rms_norm(oa, gdn_norm_w.astype(f32)) * jax.nn.silu(z.astype(f32).reshape(B, L, H_A, DV_A))
    qb = qm.astype(f32).reshape(B, L, H_B, DQK_B)
    kb = km.astype(f32).reshape(B, L, H_B, DQK_B) * (DQK_B ** -0.5)
    vb = vm.astype(f32).reshape(B, L, H_B, DV_B)
    ig = im.astype(f32) + b_i.astype(f32)
    lf = jax.nn.log_sigmoid(fm.astype(f32) + b_f.astype(f32))
    hb, c_new, n_new, m_new = _mlstm(qb, kb, vb, ig, lf, c_m.astype(f32), n_m.astype(f32), m_m.astype(f32))
    ob = jax.nn.sigmoid(om.astype(f32)).reshape(B, L, H_B, DV_B) * _rms_norm(hb, m_norm_w.astype(f32))
    ya = oa.reshape(B, L, GDN_V).astype(x.dtype) @ w_br_a
    yb = ob.reshape(B, L, M_V).astype(x.dtype) @ w_br_b
    mixed = jax.nn.sigmoid(ga) * ya + jax.nn.sigmoid(gb) * yb
    return mixed @ w_out, (new_buf, s_new, c_new, n_new, m_new)


def _peer(x, wq, keys, u_tab, v_tab):
    B, L, D = x.shape
    blk = PEER_BLOCK if L % PEER_BLOCK == 0 else L
    xb = x.reshape(-1, blk, D)

    def one_block(xt):
        q = (xt @ wq).reshape(blk, PEER_HEADS, 2, PEER_DKEY // 2)
        s = jnp.einsum('thpd,hpnd->thpn', q, keys).astype(jnp.float32)
        sv, si = lax.top_k(s, PEER_TOPK)
        cand = (sv[..., 0, :, None] + sv[..., 1, None, :]).reshape(blk, PEER_HEADS, PEER_TOPK * PEER_TOPK)
        cidx = (si[..., 0, :, None] * N_KEYS + si[..., 1, None, :]).reshape(blk, PEER_HEADS, PEER_TOPK * PEER_TOPK)
        top_v, top_p = lax.top_k(cand, PEER_TOPK)
        e_idx = jnp.take_along_axis(cidx, top_p, -1).reshape(blk, PEER_HEADS * PEER_TOPK)
        gate = jax.nn.softmax(top_v, -1).reshape(blk, PEER_HEADS * PEER_TOPK)
        act = jax.nn.gelu(jnp.einsum('ted,td->te', u_tab[e_idx], xt).astype(jnp.float32))
        coef = (gate * act).astype(xt.dtype)
        return jnp.einsum('te,ted->td', coef, v_tab[e_idx])

    return lax.map(one_block, xb).reshape(B, L, D)


def _layer(x, p_emb, conv_buf, s_gdn, c_m, n_m, m_m, w_in, conv_w, a_log, dt_bias, gdn_norm_w, b_i, b_f,
           m_norm_w, w_br_a, w_br_b, w_out, ln1_g, ln1_b, peer_wq, peer_keys, peer_u, peer_v,
           ln2_g, ln2_b, ple_proj, ple_gate):
    mix, states = _token_mixers(x, conv_buf, s_gdn, c_m, n_m, m_m, w_in, conv_w, a_log, dt_bias, gdn_norm_w,
                                b_i, b_f, m_norm_w, w_br_a, w_br_b, w_out)
    x = _layer_norm(ALPHA * x + mix, ln1_g, ln1_b)
    x = _layer_norm(ALPHA * x + _peer(x, peer_wq, peer_keys, peer_u, peer_v), ln2_g, ln2_b)
    x = x + jax.nn.sigmoid(x @ ple_gate) * (p_emb @ ple_proj)
    return x, states


def setup_inputs(seed: int = 0) -> dict:
    key = jax.random.key(seed)
    ks = iter(jax.random.split(key, 48))
    f32 = jnp.float32

    def nrm(shape, scale):
        return jax.random.normal(next(ks), shape, f32) * scale

    def unif(shape, lo, hi):
        return jax.random.uniform(next(ks), shape, f32, lo, hi)

    dt = jnp.exp(unif((DEPTH, H_A), float(np.log(1e-3)), float(np.log(1e-1))))
    return {
        "x_prompt": nrm((BATCH, SEQ, D_MODEL), 1.0),
        "x_sample": nrm((DEC_BATCH, DEC_SEQ, D_MODEL), 1.0),
        "state_gdn_conv": nrm((DEPTH, DEC_BATCH, CONV_W - 1, GDN_CONV_DIM), 1.0),
        "state_gdn_s": nrm((DEPTH, DEC_BATCH, H_A, DK_A, DV_A), 0.1),
        "state_mlstm_c": nrm((DEPTH, DEC_BATCH, H_B, DQK_B, DV_B), 1.0),
        "state_mlstm_n": nrm((DEPTH, DEC_BATCH, H_B, DQK_B), 1.0),
        "state_mlstm_m": nrm((DEPTH, DEC_BATCH, H_B), 1.0),
        "p_prompt": nrm((DEPTH, BATCH, SEQ, PLE_DIM), 1.0),
        "p_sample": nrm((DEPTH, DEC_BATCH, DEC_SEQ, PLE_DIM), 1.0),
        "ln0_g": 1.0 + nrm((D_MODEL,), 0.02),
        "ln0_b": nrm((D_MODEL,), 0.02),
        "w_in": nrm((DEPTH, D_MODEL, N_IN), D_MODEL ** -0.5),
        "gdn_conv_w": nrm((DEPTH, CONV_W, GDN_CONV_DIM), CONV_W ** -0.5),
        "gdn_a_log": jnp.log(unif((DEPTH, H_A), 1.0, 16.0)),
        "gdn_dt_bias": dt + jnp.log(-jnp.expm1(-dt)),
        "gdn_norm_w": 1.0 + nrm((DEPTH, DV_A), 0.02),
        "mlstm_b_i": nrm((DEPTH, H_B), 0.1),
        "mlstm_b_f": unif((DEPTH, H_B), 3.0, 6.0),
        "mlstm_norm_w": 1.0 + nrm((DEPTH, DV_B), 0.02),
        "w_branch_a": nrm((DEPTH, GDN_V, D_MODEL), BETA * GDN_V ** -0.5),
        "w_branch_b": nrm((DEPTH, M_V, D_MODEL), BETA * M_V ** -0.5),
        "w_out": nrm((DEPTH, D_MODEL, D_MODEL), BETA * D_MODEL ** -0.5),
        "ln1_g": 1.0 + nrm((DEPTH, D_MODEL), 0.02),
        "ln1_b": nrm((DEPTH, D_MODEL), 0.02),
        "peer_wq": nrm((DEPTH, D_MODEL, PEER_HEADS * PEER_DKEY), D_MODEL ** -0.5),
        "peer_keys": nrm((DEPTH, PEER_HEADS, 2, N_KEYS, PEER_DKEY // 2), (PEER_DKEY // 2) ** -0.5),
        "peer_u": nrm((DEPTH, N_EXPERTS, D_MODEL), D_MODEL ** -0.5),
        "peer_v": nrm((DEPTH, N_EXPERTS, D_MODEL), BETA * PEER_HEADS ** -0.5),
        "ln2_g": 1.0 + nrm((DEPTH, D_MODEL), 0.02),
        "ln2_b": nrm((DEPTH, D_MODEL), 0.02),
        "ple_proj": nrm((DEPTH, PLE_DIM, D_MODEL), PLE_DIM ** -0.5),
        "ple_gate": nrm((DEPTH, D_MODEL, D_MODEL), D_MODEL ** -0.5),
    }


def reference(x_prompt, x_sample, state_gdn_conv, state_gdn_s, state_mlstm_c, state_mlstm_n, state_mlstm_m,
              p_prompt, p_sample, ln0_g, ln0_b, w_in, gdn_conv_w, gdn_a_log, gdn_dt_bias, gdn_norm_w,
              mlstm_b_i, mlstm_b_f, mlstm_norm_w, w_branch_a, w_branch_b, w_out, ln1_g, ln1_b,
              peer_wq, peer_keys, peer_u, peer_v, ln2_g, ln2_b, ple_proj, ple_gate):
    def run(x, p, conv0, s0, c0, n0, m0):
        h = _layer_norm(x, ln0_g, ln0_b)
        new = ([], [], [], [], [])
        for i in range(DEPTH):
            h, st = _layer(h, p[i], conv0[i], s0[i], c0[i], n0[i], m0[i],
                           w_in[i], gdn_conv_w[i], gdn_a_log[i], gdn_dt_bias[i], gdn_norm_w[i],
                           mlstm_b_i[i], mlstm_b_f[i], mlstm_norm_w[i], w_branch_a[i], w_branch_b[i], w_out[i],
                           ln1_g[i], ln1_b[i], peer_wq[i], peer_keys[i], peer_u[i], peer_v[i],
                           ln2_g[i], ln2_b[i], ple_proj[i], ple_gate[i])
            for lst, s in zip(new, st):
                lst.append(s.astype(x.dtype))
        return h, [jnp.stack(l) for l in new]

    bp = x_prompt.shape[0]
    dtp = x_prompt.dtype
    y_prompt, (pc, ps, pC, pn, pm) = run(
        x_prompt, p_prompt,
        jnp.zeros((DEPTH, bp, CONV_W - 1, GDN_CONV_DIM), dtp),
        jnp.zeros((DEPTH, bp, H_A, DK_A, DV_A), dtp),
        jnp.zeros((DEPTH, bp, H_B, DQK_B, DV_B), dtp),
        jnp.zeros((DEPTH, bp, H_B, DQK_B), dtp),
        jnp.zeros((DEPTH, bp, H_B), dtp))
    y_sample, (sc, ss, sC, sn, sm) = run(
        x_sample, p_sample, state_gdn_conv, state_gdn_s, state_mlstm_c, state_mlstm_n, state_mlstm_m)
    return (y_prompt, y_sample, pc, ps, pC, pn, pm, sc, ss, sC, sn, sm)
```

```python
import numpy as np
from contextlib import ExitStack
import concourse.bass as bass
import concourse.mybir as mybir
from concourse.bass_utils import run_bass_kernel_spmd

F32 = mybir.dt.float32
BF16 = mybir.dt.bfloat16
DELTA = 1e-5
AF = mybir.ActivationFunctionType
ALU = mybir.AluOpType
AX = mybir.AxisListType

D = 2048
NIN = 18480
HA = 16
HB = 8
ALPHA = 2.0 ** 0.25
LN_EPS = 1e-5
RMS_EPS = 1e-6
NEG = -30000.0
NPTM = 10288
ENGS = ('pe', 'dve', 'act', 'pool', 'sp')
KRING = 12
GEN = 30000


class Prog:
    def __init__(self, nc):
        self.nc = nc
        self.ops = []
        self.lastw = {}
        self.readers = {}
        self.last_eng = {}
        self.dmas = []

    max_ops = None

    def add(self, eng, meth, kw, reads=(), writes=(), dma=False):
        idx = len(self.ops)
        if Prog.max_ops is not None and idx >= Prog.max_ops:
            return idx
        psr = [k for k in reads if isinstance(k, tuple) and k[0] == 'ps']
        if psr:
            reads = [k for k in reads if k not in psr]
            writes = list(writes) + psr
        deps = set()
        for k in reads:
            w = self.lastw.get(k)
            if w is not None:
                deps.add(w)
        for k in writes:
            w = self.lastw.get(k)
            if w is not None:
                deps.add(w)
            for r in self.readers.get(k, ()):
                deps.add(r)
        if eng == 'pe' and not dma:
            deps = {d for d in deps if not (self.ops[d][0] == 'pe' and not self.ops[d][4])}
        self.ops.append([eng, (meth, kw), deps, False, dma, None])
        for k in reads:
            self.readers.setdefault(k, []).append(idx)
        for k in writes:
            self.lastw[k] = idx
            self.readers[k] = []
        if dma:
            self.dmas.append(idx)
        else:
            self.last_eng[eng] = idx
        return idx

    def barrier(self):
        deps = set(self.last_eng.values()) | set(self.dmas)
        for e in ENGS:
            self.ops.append([e, None, set(deps), False, False, None])
        self.dmas = []
        self.lastw = {}
        self.readers = {}

    def emit(self, st):
        nc = self.nc
        eo = {'pe': nc.tensor, 'dve': nc.vector, 'act': nc.scalar, 'pool': nc.gpsimd, 'sp': nc.sync}
        ops = self.ops
        for op in ops:
            for d in op[2]:
                ops[d][3] = True
        nsig = {e: 0 for e in ENGS}
        for op in ops:
            if op[3] and not op[4]:
                nsig[op[0]] += 1
        sems = []
        csem = {}
        for e in ENGS:
            csem[e] = []
            for g in range(nsig[e] // GEN + 1):
                s = st.enter_context(nc.semaphore(f"c_{e}_{g}"))
                csem[e].append(len(sems))
                sems.append(s)
        dsem = {}
        for e in ('sp', 'act', 'pool'):
            dsem[e] = []
            for i in range(KRING):
                s = st.enter_context(nc.semaphore(f"d_{e}_{i}"))
                dsem[e].append(len(sems))
                sems.append(s)
        cnt = {e: 0 for e in ENGS}
        dcnt = {e: 0 for e in ENGS}
        known = {e: {} for e in ENGS}
        for op in ops:
            eng, fn, deps, sig, dma, _ = op
            e = eo[eng]
            need = {}
            for d in deps:
                s, v = ops[d][5]
                if need.get(s, 0) < v:
                    need[s] = v
            if dma:
                n = dcnt[eng]
                slot = n % KRING
                val = 16 * (n // KRING + 1)
                s = dsem[eng][slot]
                if n >= KRING and need.get(s, 0) < val - 16:
                    need[s] = val - 16
                dcnt[eng] += 1
                op[5] = (s, val)
            kn = known[eng]
            for s, v in need.items():
                if kn.get(s, 0) < v:
                    e.wait_ge(sems[s], v)
                    kn[s] = v
            if fn is None:
                continue
            ins = getattr(e, fn[0])(**fn[1])
            if dma:
                ins.then_inc(sems[op[5][0]], 16)
            elif sig:
                c = cnt[eng]
                cnt[eng] += 1
                g = c // GEN
                op[5] = (csem[eng][g], c % GEN + 1)
                ins.then_inc(sems[csem[eng][g]], 1)
        for q in ('sp', 'act', 'pool'):
            n = dcnt[q]
            for slot in range(min(n, KRING)):
                last = ((n - 1 - slot) // KRING) * KRING + slot
                val = 16 * (last // KRING + 1)
                s = dsem[q][slot]
                if known['sp'].get(s, 0) < val:
                    nc.sync.wait_ge(sems[s], val)
                    known['sp'][s] = val


class Rot:
    def __init__(self, items):
        self.items = items
        self.i = 0

    def next(self):
        it = self.items[self.i % len(self.items)]
        self.i += 1
        return it


def build(NP, LP, NS, LS, debug_stage=None):
    nc = bass.Bass("TRN2", target_bir_lowering=False)
    P = Prog(nc)
    NQ = NP + NS
    LTOT = NP * LP + NS * LS
    seqs = []
    off = 0
    for i in range(NP):
        seqs.append(dict(off=off, L=LP, T=min(128, LP), sample=None, q=i))
        off += LP
    for i in range(NS):
        seqs.append(dict(off=off, L=LS, T=LS, sample=i, q=NP + i))
        off += LS
    for s in seqs:
        s['cb'] = s['off'] + 3 * s['q']
    NCOLS = LTOT + 3 * NQ

    def din(name, shape):
        return nc.dram_tensor(name, list(shape), F32, kind="ExternalInput").ap()

    def dout(name, shape):
        return nc.dram_tensor(name, list(shape), F32, kind="ExternalOutput").ap()

    def dscr(name, shape, dbg=False):
        kind = "ExternalOutput" if dbg else "Internal"
        return nc.dram_tensor(name, list(shape), F32, kind=kind).ap()

    x_d = din("x", [LTOT, D])
    p_d = din("p", [LTOT, 256])
    NSS = max(NS, 1)
    sconv_d = din("sconv", [NSS, 3, 6144])
    ss_d = din("ss", [NSS, 16, 128, 128])
    sc_d = din("sc", [NSS, 8, 128, 256])
    sn_d = din("sn", [NSS, 8, 128])
    sm_d = din("sm", [NSS, 8])
    ln0g_d = din("ln0_g", [D]); ln0b_d = din("ln0_b", [D])
    win_d = din("w_in", [D, NIN])
    convw_d = din("gdn_conv_w", [4, 6144])
    alog_d = din("gdn_a_log", [16]); dtb_d = din("gdn_dt_bias", [16])
    gnw_d = din("gdn_norm_w", [128])
    bi_d = din("mlstm_b_i", [8]); bf_d = din("mlstm_b_f", [8])
    mnw_d = din("mlstm_norm_w", [256])
    wa_d = din("w_branch_a", [D, D]); wb_d = din("w_branch_b", [D, D]); wo_d = din("w_out", [D, D])
    ln1g_d = din("ln1_g", [D]); ln1b_d = din("ln1_b", [D])
    wq_d = din("peer_wq", [D, D])
    keys_d = din("peer_keys", [16, 128, 128])
    pu_d = din("peer_u", [16384, D]); pv_d = din("peer_v", [16384, D])
    ln2g_d = din("ln2_g", [D]); ln2b_d = din("ln2_b", [D])
    wp_d = din("ple_proj", [256, D]); wg_d = din("ple_gate", [D, D])
    cst_d = din("consts", [128, 5 * 128])

    y_d = dout("y", [LTOT, D])
    oconv_d = dout("oconv", [NQ, 3, 6144])
    os_d = dout("os", [NQ, 16, 128, 128])
    oc_d = dout("oc", [NQ, 8, 128, 256])
    on_d = dout("on", [NQ, 8, 128])
    om_d = dout("om", [NQ, 8])

    dbg = debug_stage is not None
    H_s = dscr("H_s", [LTOT, D], dbg)
    QKVT_s = dscr("QKVT_s", [8192, NCOLS], dbg)
    PTM_s = dscr("PTM_s", [LTOT, NPTM], dbg)
    OABT_s = dscr("OABT_s", [2, 16, 128, LTOT], dbg)
    X1_s = dscr("X1_s", [LTOT, D], dbg)
    X1T_s = dscr("X1T_s", [16, 128, LTOT], dbg)
    PG_s = dscr("PG_s", [LTOT, 2, 1024], dbg)
    PGT_s = dscr("PGT_s", [LTOT, 24], dbg)

    with ExitStack() as top:
        uniq = [0]

        def sb(st, name, shape):
            uniq[0] += 1
            return st.enter_context(nc.sbuf_tensor(f"{name}_{uniq[0]}", list(shape), F32))

        def sb16(st, name, shape):
            uniq[0] += 1
            return st.enter_context(nc.sbuf_tensor(f"{name}_{uniq[0]}", list(shape), BF16))

        psb = [top.enter_context(nc.psum_tensor(f"ps{i}", [128, 512], F32)) for i in range(8)]
        psrot = Rot(list(range(8)))

        def ps():
            i = psrot.next()
            return psb[i], ('ps', i)

        def A(eng, meth, reads, writes, **kw):
            return P.add(eng, meth, kw, reads, writes)

        def DMA(q, out, in_, reads=(), writes=()):
            return P.add(q, 'dma_start', dict(out=out, in_=in_), reads, writes, dma=True)

        cst = sb(top, "cst", [128, 640])
        DMA('sp', cst[:], cst_d[:, :], writes=['cst'])
        ident = cst[:, 0:128]
        triU = cst[:, 128:256]
        NEGI = cst[:, 256:384]
        NEGS = cst[:, 384:512]
        NEGTI = cst[:, 512:640]
        ones = sb(top, "ones", [128, 128])
        A('pool', 'memset', [], ['ones'], ap=ones[:], constant=1.0)

        def bcast_load(st, name, src, n):
            t = sb(st, name, [128, n])
            DMA('sp', t[:], src.partition_broadcast(128), writes=[name])
            return t

        def layer_norm(st_tiles, xin, xin_key, T, g_t, g_key, b_t, b_key, out, out_key):
            stats, mv, sd = st_tiles
            for c in range(4):
                A('dve', 'bn_stats', [xin_key], ['ln_stats'], out=stats[:T, c, :], in_=xin[:T, c * 512:(c + 1) * 512])
            A('dve', 'bn_aggr', ['ln_stats'], ['ln_mv'], out=mv[:T, :], in_=stats[:T, :, :])
            A('act', 'activation', ['ln_mv', 'eps_ln'], ['ln_sd'], out=sd[:T, 0:1], in_=mv[:T, 1:2], func=AF.Sqrt, bias=eps_ln[:T, :], scale=1.0)
            A('dve', 'reciprocal', ['ln_sd'], ['ln_rs'], out=sd[:T, 1:2], in_=sd[:T, 0:1])
            A('dve', 'tensor_scalar', [xin_key, 'ln_mv', 'ln_rs'], [out_key], out=out[:T, :], in0=xin[:T, :],
              scalar1=mv[:T, 0:1], scalar2=sd[:T, 1:2], op0=ALU.subtract, op1=ALU.mult)
            A('pool', 'tensor_tensor', [out_key, g_key], [out_key], out=out[:T, :], in0=out[:T, :], in1=g_t[:T, :], op=ALU.mult)
            A('pool', 'tensor_tensor', [out_key, b_key], [out_key], out=out[:T, :], in0=out[:T, :], in1=b_t[:T, :], op=ALU.add)

        eps_ln = sb(top, "eps_ln", [128, 1])
        A('pool', 'memset', [], ['eps_ln'], ap=eps_ln[:], constant=LN_EPS)

        def transpose_to(src, src_key, T, dstT, dst_key, col0, nkc=16):
            for g in range(0, nkc, 4):
                pt, pk = ps()
                n = min(4, nkc - g)
                for j in range(n):
                    kc = g + j
                    A('pe', 'transpose', [src_key, 'cst'], [pk], out=pt[:, j * 128:j * 128 + T],
                      in_=src[:T, kc * 128:(kc + 1) * 128], identity=ident[:T, :T])
                A('act', 'copy', [pk], [dst_key], out=dstT[:, g:g + n, col0:col0 + T],
                  in_=pt[:, 0:n * 128].rearrange("p (j t) -> p j t", t=128)[:, :, 0:T])

        def stage1():
            with ExitStack() as st:
                g0 = bcast_load(st, "ln0g", ln0g_d, D)
                b0 = bcast_load(st, "ln0b", ln0b_d, D)
                xts = [sb(st, f"s1x{i}", [128, D]) for i in range(2)]
                hts = [sb(st, f"s1h{i}", [128, D]) for i in range(2)]
                hT = sb16(st, "s1hT", [128, 16, 512])
                wfm = [sb(st, f"s1wf{i}", [128, 16, 128]) for i in range(2)]
                wtm = [sb(st, f"s1wt{i}", [128, 16, 512]) for i in range(2)]
                wfm16 = [sb16(st, f"s1wfb{i}", [128, 16, 128]) for i in range(2)]
                wtm16 = [sb16(st, f"s1wtb{i}", [128, 16, 512]) for i in range(2)]
                ofm = [sb(st, f"s1of{i}", [128, 512]) for i in range(2)]
                otm = [sb(st, f"s1ot{i}", [128, 512]) for i in range(3)]
                lnt = (sb(st, "ln_stats", [128, 4, 6]), sb(st, "ln_mv", [128, 2]), sb(st, "ln_sd", [128, 2]))
                xr = Rot([0, 1]); hr = Rot([0, 1]); wfr = Rot([0, 1]); wtr = Rot([0, 1]); ofr = Rot([0, 1]); otr = Rot([0, 1, 2])
                groups = []
                for s in seqs:
                    if s['sample'] is None:
                        for t0 in range(0, s['L'], 512):
                            n = min(512, s['L'] - t0)
                            groups.append((s['off'] + t0, n, [(s, t0, 0, n)]))
                if NS:
                    segs = [(s, 0, i * LS, LS) for i, s in enumerate(seqs[NP:])]
                    groups.append((seqs[NP]['off'], NS * LS, segs))
                fm_cols = [c * 128 for c in range(48)] + [8224 + c * 128 for c in range(8)] + [9248 + c * 128 for c in range(8)]
                tm_blocks = []
                for c in range(4):
                    tm_blocks.append((6144 + c * 512, 512, c * 512, AF.Silu))
                for c in range(4):
                    tm_blocks.append((10272 + c * 512, 512, 2048 + c * 512, AF.Identity))
                for c in range(4):
                    tm_blocks.append((12320 + c * 512, 512, 4096 + c * 512, AF.Sigmoid))
                for c in range(8):
                    tm_blocks.append((14384 + c * 512, 512, 6144 + c * 512, AF.Sigmoid))
                tm_blocks.append((8192, 16, 10240, AF.Sigmoid))
                tm_blocks.append((8208, 16, 10256, AF.Identity))
                tm_blocks.append((14368, 16, 10272, AF.Identity))
                win_v = win_d.rearrange("(kc kp) c -> kp kc c", kp=128)
                for (tok0, ntok, segs) in groups:
                    tiles = [(a, min(128, ntok - a)) for a in range(0, ntok, 128)]
                    for (a, T) in tiles:
                        xi = xr.next(); hi = hr.next()
                        xt = xts[xi]; ht = hts[hi]
                        DMA('sp', xt[:T, :], x_d[tok0 + a:tok0 + a + T, :], writes=[('s1x', xi)])
                        layer_norm(lnt, xt, ('s1x', xi), T, g0, "ln0g", b0, "ln0b", ht, ('s1h', hi))
                        DMA('sp', H_s[tok0 + a:tok0 + a + T, :], ht[:T, :], reads=[('s1h', hi)])
                        transpose_to(ht, ('s1h', hi), T, hT, 's1hT', a)
                    for bi, c0 in enumerate(fm_cols):
                        wi = wfr.next(); w = wfm[wi]
                        DMA('act', w[:], win_v[:, :, c0:c0 + 128], writes=[('s1wf', wi)])
                        A('pool', 'tensor_copy', [('s1wf', wi)], [('s1wfb', wi)], out=wfm16[wi][:], in_=w[:])
                        w = wfm16[wi]
                        pt, pk = ps()
                        for kc in range(16):
                            A('pe', 'matmul', [('s1wfb', wi), 's1hT'], [pk], out=pt[:, :ntok], lhsT=w[:, kc, :], rhs=hT[:, kc, :ntok],
                              start=(kc == 0), stop=(kc == 15))
                        oi = ofr.next(); o = ofm[oi]
                        A('dve', 'tensor_copy', [pk], [('s1of', oi)], out=o[:, :ntok], in_=pt[:, :ntok])
                        for (s, t0, gc, n) in segs:
                            col = s['cb'] + 3 + t0
                            DMA('sp', QKVT_s[bi * 128:(bi + 1) * 128, col:col + n], o[:, gc:gc + n], reads=[('s1of', oi)])
                    for (c0, ncol, pc0, func) in tm_blocks:
                        wi = wtr.next(); w = wtm[wi]
                        DMA('act', w[:, :, :ncol], win_v[:, :, c0:c0 + ncol], writes=[('s1wt', wi)])
                        A('dve', 'tensor_copy', [('s1wt', wi)], [('s1wtb', wi)], out=wtm16[wi][:, :, :ncol], in_=w[:, :, :ncol])
                        w = wtm16[wi]
                        for (a, T) in tiles:
                            pt, pk = ps()
                            for kc in range(16):
                                A('pe', 'matmul', [('s1wtb', wi), 's1hT'], [pk], out=pt[:T, :ncol], lhsT=hT[:, kc, a:a + T],
                                  rhs=w[:, kc, :ncol], start=(kc == 0), stop=(kc == 15))
                            oi = otr.next(); o = otm[oi]
                            A('act', 'activation', [pk], [('s1ot', oi)], out=o[:T, :ncol], in_=pt[:T, :ncol], func=func)
                            DMA('sp', PTM_s[tok0 + a:tok0 + a + T, pc0:pc0 + ncol], o[:T, :ncol], reads=[('s1ot', oi)])
                P.barrier()

        def DMAs(q, out, in_, reads=(), writes=()):
            return P.add(q, 'dma_start', dict(out=out, in_=in_, allow_slow_non_contiguous=True), reads, writes, dma=True)

        def stage2():
            with ExitStack() as st:
                cw = sb(st, "cw", [128, 48, 4])
                craw = sb(st, "craw", [4, 6144])
                cpad = sb(st, "cpad", [128, 48, 3])
                ocv = craw
                snr = sb(st, "snr", [8, 128])
                DMA('sp', craw[:, :], convw_d[:, :], writes=['craw'])
                pt, pk = ps()
                for blk in range(48):
                    A('pe', 'matmul', ['craw', 'cst'], [pk], out=pt[:, blk * 4:blk * 4 + 4], lhsT=craw[0:4, blk * 128:(blk + 1) * 128], rhs=ident[0:4, 0:4], start=True, stop=True)
                A('act', 'copy', [pk], ['cw'], out=cw[:, :, :], in_=pt[:, 0:192].rearrange("p (b w) -> p b w", w=4))
                dtb = bcast_load(st, "dtb", dtb_d, 16)
                aexp = bcast_load(st, "aexp", alog_d, 16)
                A('act', 'activation', ['aexp'], ['aexp'], out=aexp[:], in_=aexp[:], func=AF.Exp)
                gnw = bcast_load(st, "gnw", gnw_d, 128)
                mnw = bcast_load(st, "mnw", mnw_d, 256)
                bi = bcast_load(st, "bi", bi_d, 8)
                bf = bcast_load(st, "bf", bf_d, 8)
                eps6 = sb(st, "eps6", [128, 1]); A('pool', 'memset', [], ['eps6'], ap=eps6[:], constant=1e-6)
                one1 = sb(st, "one1", [128, 1]); A('pool', 'memset', [], ['one1'], ap=one1[:], constant=1.0)
                zt = sb(st, "zt", [128, D]); vmt = sb(st, "vmt", [128, D]); omt = sb(st, "omt", [128, D])
                sm48 = sb(st, "sm48", [128, 48])
                win = sb(st, "win", [128, 16, 131])
                qmk = sb(st, "qmk", [128, 16, 128])
                cv = [sb(st, f"cv{i}", [128, 16, 128]) for i in range(3)]
                ctmp = sb(st, "ctmp", [128, 16, 128])
                ktm = sb(st, "ktm", [128, 16, 128]); vtm = sb(st, "vtm", [128, 16, 128])
                S = sb(st, "S", [128, 16, 128]); Cx = sb(st, "Cx", [128, 8, 257]); mrep = sb(st, "mrep", [128, 8])
                Gbc = sb(st, "Gbc", [128, 16, 128]); eGbc = sb(st, "eGbc", [128, 16, 128]); grep = sb(st, "grep", [128, 16, 128])
                sm = sb(st, "smallt", [128, 256])
                g1 = sm[:, 0:16]; Gtm = sm[:, 16:32]; negG = sm[:, 32:48]; bexpG = sm[:, 48:64]; kdsc = sm[:, 64:80]
                nbeta = sm[:, 80:96]; ssq = sm[:, 96:112]; ig = sm[:, 112:120]; lf = sm[:, 120:128]; nlf = sm[:, 128:136]
                Ftm = sm[:, 136:144]; rr = sm[:, 144:152]; ssqb = sm[:, 152:160]; ksc = sm[:, 160:168]
                V3 = sb(st, "V3", [128, 24])
                hw = sb(st, "hwork", [128, 16, 128])
                Es = hw[:, 0, :]; ETi = hw[:, 1, :]; w1 = hw[:, 2, :]; w2 = hw[:, 3, :]; attnT = hw[:, 4, :]
                Pb = [hw[:, 5, :], hw[:, 6, :]]; PTb = [hw[:, 7, :], hw[:, 8, :]]; TTb = [hw[:, 9, :], hw[:, 10, :]]
                rv = hw[:, 11, :]; rk = hw[:, 12, :]; wT = hw[:, 13, :]; upre = hw[:, 14, :]; usb = hw[:, 15, :]
                hw2 = sb(st, "hwork2", [128, 8, 128])
                qg = hw2[:, 0, :]; kdec = hw2[:, 1, :]; dl = hw2[:, 2, :]; ww = hw2[:, 3, :]; wi = hw2[:, 4, :]; wiT = hw2[:, 5, :]
                kend = hw2[:, 6, :]; sel = hw2[:, 7, :]
                nq = sb(st, "nq", [128, 257])
                pp = sb(st, "pp", [128, 16])
                osb = sb(st, "osb", [128, D]); hsb = sb(st, "hsb", [128, D])
                oab = [sb(st, "oa", [128, D]), sb(st, "ob", [128, D])]
                wz = zt
                oT = ctmp
                lastbc = sb(st, "lastbc", [128, 48])
                igrep = grep[:, 0:8, :]; nlfrep = grep[:, 8:16, :]

                def hk(name, h=None):
                    return name if h is None else (name, h)

                for s in seqs:
                    T = s['T']; L = s['L']; q = s['q']; smp = s['sample']
                    nsq = {128: 6, 64: 5, 32: 4, 16: 3}[T]
                    if smp is None:
                        A('pool', 'memset', [], ['S'], ap=S[:], constant=0.0)
                        A('pool', 'memset', [], ['Cx'], ap=Cx[:], constant=0.0)
                        A('pool', 'memset', [], ['mrep'], ap=mrep[:], constant=0.0)
                    else:
                        DMA('sp', S[:], ss_d[smp].rearrange("h k v -> k h v"), writes=['S'])
                        DMA('sp', Cx[:, :, 0:256], sc_d[smp].rearrange("h k v -> k h v"), writes=['Cx'])
                        DMA('sp', snr[:, :], sn_d[smp], writes=['snr'])
                        pt, pk = ps()
                        A('pe', 'matmul', ['snr', 'cst'], [pk], out=pt[:, 0:8], lhsT=snr[0:8, :], rhs=ident[0:8, 0:8], start=True, stop=True)
                        A('act', 'copy', [pk], ['Cx'], out=Cx[:, :, 256], in_=pt[:, 0:8])
                        DMA('sp', craw[0:3, :], sconv_d[smp], reads=[], writes=['craw'])
                        pt, pk = ps()
                        for blk in range(48):
                            A('pe', 'matmul', ['craw', 'cst'], [pk], out=pt[:, blk * 3:blk * 3 + 3], lhsT=craw[0:3, blk * 128:(blk + 1) * 128], rhs=ident[0:3, 0:3], start=True, stop=True)
                        A('act', 'copy', [pk], ['cpad'], out=cpad[:, :, :], in_=pt[:, 0:144].rearrange("p (b w) -> p b w", w=3))
                        DMA('sp', mrep[:], sm_d[smp].partition_broadcast(128), writes=['mrep'])
                    A('pool', 'tensor_copy', ['cst'], ['sel'], out=sel[:T, :], in_=ident[:T, T - 1:T].to_broadcast([T, 128]))
                    for k in range(L // T):
                        tok0 = s['off'] + k * T
                        col0 = s['cb'] + 3 + k * T
                        DMA('sp', zt[:T, :], PTM_s[tok0:tok0 + T, 0:2048], writes=['zt'])
                        DMA('sp', vmt[:T, :], PTM_s[tok0:tok0 + T, 2048:4096], writes=['vmt'])
                        DMA('sp', omt[:T, :], PTM_s[tok0:tok0 + T, 4096:6144], writes=['omt'])
                        DMA('sp', sm48[:T, :], PTM_s[tok0:tok0 + T, 10240:10288], writes=['sm48'])
                        beta = sm48[:, 0:16]
                        DMA('act', qmk[:, :, :T], QKVT_s[6144:8192, col0:col0 + T].rearrange("(b p) c -> p b c", p=128), writes=['qmk'])
                        for gi in range(3):
                            b0 = gi * 16
                            src = QKVT_s[b0 * 128:(b0 + 16) * 128, :].rearrange("(b p) c -> p b c", p=128)
                            if k == 0:
                                if smp is None:
                                    A('pool', 'memset', [], ['win'], ap=win[:, :, 0:3], constant=0.0)
                                else:
                                    A('pool', 'tensor_copy', ['cpad'], ['win'], out=win[:, :, 0:3], in_=cpad[:, b0:b0 + 16, :])
                                DMA('act', win[:, :, 3:3 + T], src[:, :, col0:col0 + T], writes=['win'])
                            else:
                                DMA('act', win[:, :, 0:3 + T], src[:, :, col0 - 3:col0 + T], writes=['win'])
                            if k == L // T - 1:
                                for g4 in range(0, 16, 4):
                                    pt, pk = ps()
                                    for j in range(4):
                                        A('pe', 'matmul', ['win', 'cst'], [pk], out=pt[0:3, j * 128:(j + 1) * 128], lhsT=win[:, g4 + j, T:T + 3], rhs=ident[:, :], start=True, stop=True)
                                    A('act', 'copy', [pk], ['craw'], out=ocv[0:3, (b0 + g4) * 128:(b0 + g4 + 4) * 128], in_=pt[0:3, :])
                            dst = cv[gi]; dk = f"cv{gi}"
                            A('dve', 'tensor_tensor', ['win', 'cw'], [dk], out=dst[:, :, :T], in0=win[:, :, 0:T],
                              in1=cw[:, b0:b0 + 16, 0:1].to_broadcast([128, 16, T]), op=ALU.mult)
                            for w in range(1, 4):
                                A('pool', 'tensor_tensor', ['win', 'cw'], ['ctmp'], out=ctmp[:, :, :T], in0=win[:, :, w:w + T],
                                  in1=cw[:, b0:b0 + 16, w:w + 1].to_broadcast([128, 16, T]), op=ALU.mult)
                                A('dve', 'tensor_tensor', [dk, 'ctmp'], [dk], out=dst[:, :, :T], in0=dst[:, :, :T], in1=ctmp[:, :, :T], op=ALU.add)
                            A('act', 'activation', [dk], [dk], out=dst[:, :, :T], in_=dst[:, :, :T], func=AF.Silu)
                        for gi in range(2):
                            dst = cv[gi]; dk = f"cv{gi}"
                            A('dve', 'tensor_tensor', [dk], ['ctmp'], out=ctmp[:, :, :T], in0=dst[:, :, :T], in1=dst[:, :, :T], op=ALU.mult)
                            for g4 in range(0, 16, 4):
                                pt, pk = ps()
                                for j in range(4):
                                    A('pe', 'matmul', ['ones', 'ctmp'], [pk], out=pt[:, j * T:(j + 1) * T], lhsT=ones[:, :], rhs=ctmp[:, g4 + j, :T],
                                      start=True, stop=True)
                                A('act', 'activation', [pk, 'eps6'], ['ctmp'], out=ctmp[:, g4:g4 + 4, :T],
                                  in_=pt[:, 0:4 * T].rearrange("p (j t) -> p j t", t=T), func=AF.Sqrt, bias=eps6[:, :], scale=1.0)
                            A('dve', 'reciprocal', ['ctmp'], ['ctmp'], out=ctmp[:, :, :T], in_=ctmp[:, :, :T])
                            A('dve', 'scalar_tensor_tensor', [dk, 'ctmp'], [dk], out=dst[:, :, :T], in0=dst[:, :, :T],
                              scalar=(128.0 ** -0.5 if gi == 0 else 1.0), in1=ctmp[:, :, :T], op0=ALU.mult, op1=ALU.mult)
                        for (srcT, sk, dstm, dkey) in ((cv[1], 'cv1', ktm, 'ktm'), (cv[2], 'cv2', vtm, 'vtm')):
                            for g4 in range(0, 16, 4):
                                pt, pk = ps()
                                for j in range(4):
                                    A('pe', 'transpose', [sk, 'cst'], [pk], out=pt[:T, j * 128:(j + 1) * 128], in_=srcT[:, g4 + j, :T], identity=ident[:, :])
                                A('act', 'copy', [pk], [dkey], out=dstm[:T, g4:g4 + 4, :], in_=pt[:T, :].rearrange("t (j d) -> t j d", d=128))
                        A('dve', 'tensor_tensor', ['sm48', 'dtb'], ['g1'], out=g1[:T, :], in0=sm48[:T, 16:32], in1=dtb[:T, :], op=ALU.add)
                        A('act', 'activation', ['g1'], ['g1'], out=g1[:T, :], in_=g1[:T, :], func=AF.Exp)
                        A('act', 'activation', ['g1', 'one1'], ['g1'], out=g1[:T, :], in_=g1[:T, :], func=AF.Ln, bias=one1[:T, :], scale=1.0)
                        A('dve', 'scalar_tensor_tensor', ['g1', 'aexp'], ['g1'], out=g1[:T, :], in0=g1[:T, :], scalar=-1.0, in1=aexp[:T, :],
                          op0=ALU.mult, op1=ALU.mult)
                        A('pool', 'tensor_copy', ['g1'], ['grep'], out=grep[:T, :, :], in_=g1[:T, :].unsqueeze(2).to_broadcast([T, 16, 128]))
                        pt, pk = ps()
                        A('pe', 'matmul', ['cst', 'g1'], [pk], out=pt[:T, 0:16], lhsT=triU[:T, :T], rhs=g1[:T, :], start=True, stop=True)
                        A('dve', 'tensor_copy', [pk], ['Gtm'], out=Gtm[:T, :], in_=pt[:T, 0:16])
                        for g4 in range(0, 16, 4):
                            pt, pk = ps()
                            for j in range(4):
                                A('pe', 'matmul', ['grep', 'cst'], [pk], out=pt[:, j * T:(j + 1) * T], lhsT=grep[:T, g4 + j, :], rhs=triU[:T, :T],
                                  start=True, stop=True)
                            A('dve', 'tensor_copy', [pk], ['Gbc'], out=Gbc[:, g4:g4 + 4, :T], in_=pt[:, 0:4 * T].rearrange("p (j t) -> p j t", t=T))
                            A('act', 'activation', [pk], ['eGbc'], out=eGbc[:, g4:g4 + 4, :T], in_=pt[:, 0:4 * T].rearrange("p (j t) -> p j t", t=T),
                              func=AF.Exp)
                        A('dve', 'tensor_scalar', ['Gtm'], ['negG'], out=negG[:T, :], in0=Gtm[:T, :], scalar1=-1.0, scalar2=None, op0=ALU.mult)
                        A('dve', 'tensor_scalar', ['sm48'], ['nbeta'], out=nbeta[:T, :], in0=beta[:T, :], scalar1=-1.0, scalar2=None, op0=ALU.mult)
                        A('act', 'activation', ['Gtm'], ['bexpG'], out=bexpG[:T, :], in_=Gtm[:T, :], func=AF.Exp)
                        A('dve', 'tensor_tensor', ['bexpG', 'sm48'], ['bexpG'], out=bexpG[:T, :], in0=bexpG[:T, :], in1=beta[:T, :], op=ALU.mult)
                        A('dve', 'tensor_tensor', ['Gbc', 'Gtm'], ['kdsc'], out=kdsc[:T, :], in0=Gbc[:T, :, T - 1], in1=Gtm[:T, :], op=ALU.subtract)
                        A('act', 'activation', ['kdsc'], ['kdsc'], out=kdsc[:T, :], in_=kdsc[:T, :], func=AF.Exp)
                        A('pool', 'tensor_tensor', ['zt', 'gnw'], ['zt'], out=wz[:T, :].rearrange("t (h v) -> t h v", v=128),
                          in0=zt[:T, :].rearrange("t (h v) -> t h v", v=128), in1=gnw[:T, :].unsqueeze(1).to_broadcast([T, 16, 128]), op=ALU.mult)
                        for h in range(16):
                            kT = cv[1][:, h, :T]; qT = cv[0][:, h, :T]
                            p1, k1 = ps()
                            A('pe', 'matmul', ['cv1'], [k1], out=p1[:T, :T], lhsT=kT, rhs=kT, start=True, stop=True)
                            p2, k2 = ps()
                            A('pe', 'matmul', ['cv1', 'cv0'], [k2], out=p2[:T, :T], lhsT=kT, rhs=qT, start=True, stop=True)
                            A('dve', 'scalar_tensor_tensor', ['Gbc', 'cst'], ['w1'], out=w1[:T, :T], in0=Gbc[:T, h, :T], scalar=-1.0, in1=NEGS[:T, :T],
                              op0=ALU.mult, op1=ALU.add)
                            A('act', 'activation', ['w1', 'Gtm'], ['Es'], out=Es[:T, :T], in_=w1[:T, :T], func=AF.Exp, bias=Gtm[:T, h:h + 1], scale=1.0)
                            A('dve', 'tensor_tensor', ['Gbc', 'cst'], ['w2'], out=w2[:T, :T], in0=Gbc[:T, h, :T], in1=NEGTI[:T, :T], op=ALU.add)
                            A('act', 'activation', ['w2', 'negG'], ['ETi'], out=ETi[:T, :T], in_=w2[:T, :T], func=AF.Exp, bias=negG[:T, h:h + 1], scale=1.0)
                            A('dve', 'scalar_tensor_tensor', [k1, 'nbeta', 'Es'], ['P0'], out=Pb[0][:T, :T], in0=p1[:T, :T], scalar=nbeta[:T, h:h + 1],
                              in1=Es[:T, :T], op0=ALU.mult, op1=ALU.mult)
                            A('dve', 'tensor_tensor', [k2, 'ETi'], ['attnT'], out=attnT[:T, :T], in0=p2[:T, :T], in1=ETi[:T, :T], op=ALU.mult)
                            p3, k3 = ps()
                            A('pe', 'transpose', ['P0', 'cst'], [k3], out=p3[:T, :T], in_=Pb[0][:T, :T], identity=ident[:T, :T])
                            A('act', 'copy', [k3], ['PT0'], out=PTb[0][:T, :T], in_=p3[:T, :T])
                            A('dve', 'tensor_tensor', [k3, 'cst'], ['TT0'], out=TTb[0][:T, :T], in0=p3[:T, :T], in1=ident[:T, :T], op=ALU.add)
                            cur = 0
                            for m in range(1, nsq + 1):
                                nx = 1 - cur
                                pa, ka = ps()
                                A('pe', 'matmul', [f'P{cur}', f'PT{cur}'], [ka], out=pa[:T, :T], lhsT=PTb[cur][:T, :T], rhs=Pb[cur][:T, :T], start=True, stop=True)
                                if m < nsq:
                                    pb_, kb_ = ps()
                                    A('pe', 'matmul', [f'P{cur}', f'PT{cur}'], [kb_], out=pb_[:T, :T], lhsT=Pb[cur][:T, :T], rhs=PTb[cur][:T, :T], start=True, stop=True)
                                A('act', 'copy', [ka], [f'P{nx}'], out=Pb[nx][:T, :T], in_=pa[:T, :T])
                                if m < nsq:
                                    A('dve', 'tensor_copy', [kb_], [f'PT{nx}'], out=PTb[nx][:T, :T], in_=pb_[:T, :T])
                                pc, kc_ = ps()
                                A('pe', 'matmul', [f'P{nx}', f'TT{cur}'], [kc_], out=pc[:T, :T], lhsT=Pb[nx][:T, :T], rhs=TTb[cur][:T, :T], start=True, stop=True)
                                A('dve', 'tensor_tensor', [kc_, f'TT{cur}'], [f'TT{nx}'], out=TTb[nx][:T, :T], in0=pc[:T, :T], in1=TTb[cur][:T, :T], op=ALU.add)
                                cur = nx
                            TT = TTb[cur]; tk = f'TT{cur}'
                            A('pool', 'tensor_scalar', ['vtm', 'sm48'], ['rv'], out=rv[:T, :], in0=vtm[:T, h, :], scalar1=beta[:T, h:h + 1], scalar2=None, op0=ALU.mult)
                            A('pool', 'tensor_scalar', ['ktm', 'bexpG'], ['rk'], out=rk[:T, :], in0=ktm[:T, h, :], scalar1=bexpG[:T, h:h + 1], scalar2=None, op0=ALU.mult)
                            pu, ku = ps()
                            A('pe', 'matmul', [tk, 'rv'], [ku], out=pu[:T, 0:128], lhsT=TT[:T, :T], rhs=rv[:T, :], start=True, stop=True)
                            pw, kw_ = ps()
                            A('pe', 'matmul', [tk, 'rk'], [kw_], out=pw[:, :T], lhsT=rk[:T, :], rhs=TT[:T, :T], start=True, stop=True)
                            A('act', 'copy', [ku], ['upre'], out=upre[:T, :], in_=pu[:T, 0:128])
                            A('act', 'copy', [kw_], ['wT'], out=wT[:, :T], in_=pw[:, :T])
                            pws, kws = ps()
                            A('pe', 'matmul', ['wT', 'S'], [kws], out=pws[:T, 0:128], lhsT=wT[:, :T], rhs=S[:, h, :], start=True, stop=True)
                            A('dve', 'tensor_tensor', ['upre', kws], ['usb'], out=usb[:T, :], in0=upre[:T, :], in1=pws[:T, 0:128], op=ALU.subtract)
                            A('pool', 'tensor_tensor', ['cv0', 'eGbc'], ['qg'], out=qg[:, :T], in0=qT, in1=eGbc[:, h, :T], op=ALU.mult)
                            po, ko = ps()
                            A('pe', 'matmul', ['qg', 'S'], [ko], out=po[:T, 0:128], lhsT=qg[:, :T], rhs=S[:, h, :], start=True, stop=False)
                            A('pe', 'matmul', ['attnT', 'usb'], [ko], out=po[:T, 0:128], lhsT=attnT[:T, :T], rhs=usb[:T, :], start=False, stop=True)
                            A('act', 'copy', [ko], ['osb'], out=osb[:T, h * 128:(h + 1) * 128], in_=po[:T, 0:128])
                            A('pool', 'tensor_scalar', ['ktm', 'kdsc'], ['kdec'], out=kdec[:T, :], in0=ktm[:T, h, :], scalar1=kdsc[:T, h:h + 1], scalar2=None, op0=ALU.mult)
                            pS, kS = ps()
                            A('pe', 'matmul', ['kdec', 'usb'], [kS], out=pS[:, 0:128], lhsT=kdec[:T, :], rhs=usb[:T, :], start=True, stop=True)
                            A('dve', 'scalar_tensor_tensor', ['S', 'eGbc', kS], ['S'], out=S[:, h, :], in0=S[:, h, :], scalar=eGbc[:, h, T - 1:T], in1=pS[:, 0:128],
                              op0=ALU.mult, op1=ALU.add)
                        A('pool', 'tensor_tensor', ['osb'], ['hsb'], out=hsb[:T, :], in0=osb[:T, :], in1=osb[:T, :], op=ALU.mult)
                        A('dve', 'tensor_reduce', ['hsb'], ['ssq'], out=ssq[:T, :], in_=hsb[:T, :].rearrange("t (h v) -> t h v", v=128), axis=AX.X, op=ALU.add)
                        A('act', 'activation', ['ssq', 'eps6'], ['ssq'], out=ssq[:T, :], in_=ssq[:T, :], func=AF.Sqrt, bias=eps6[:T, :], scale=1.0 / 128.0)
                        A('dve', 'reciprocal', ['ssq'], ['ssq'], out=ssq[:T, :], in_=ssq[:T, :])
                        A('dve', 'tensor_tensor', ['osb', 'ssq'], ['osb'], out=osb[:T, :].rearrange("t (h v) -> t h v", v=128),
                          in0=osb[:T, :].rearrange("t (h v) -> t h v", v=128), in1=ssq[:T, :].unsqueeze(2).to_broadcast([T, 16, 128]), op=ALU.mult)
                        A('dve', 'tensor_tensor', ['osb', 'zt'], ['oa'], out=oab[0][:T, :], in0=osb[:T, :], in1=wz[:T, :], op=ALU.mult)
                        A('dve', 'tensor_tensor', ['sm48', 'bi'], ['ig'], out=ig[:T, :], in0=sm48[:T, 32:40], in1=bi[:T, :], op=ALU.add)
                        A('dve', 'tensor_tensor', ['sm48', 'bf'], ['nlf'], out=nlf[:T, :], in0=sm48[:T, 40:48], in1=bf[:T, :], op=ALU.add)
                        A('act', 'activation', ['nlf'], ['nlf'], out=nlf[:T, :], in_=nlf[:T, :], func=AF.Exp, scale=-1.0)
                        A('act', 'activation', ['nlf', 'one1'], ['nlf'], out=nlf[:T, :], in_=nlf[:T, :], func=AF.Ln, bias=one1[:T, :], scale=1.0)
                        A('dve', 'tensor_scalar', ['nlf'], ['lf'], out=lf[:T, :], in0=nlf[:T, :], scalar1=-1.0, scalar2=None, op0=ALU.mult)
                        pt, pk = ps()
                        A('pe', 'matmul', ['cst', 'lf'], [pk], out=pt[:T, 0:8], lhsT=triU[:T, :T], rhs=lf[:T, :], start=True, stop=True)
                        A('dve', 'tensor_copy', [pk], ['V3'], out=V3[:T, 16:24], in_=pt[:T, 0:8])
                        Fm = V3[:, 16:24]
                        A('dve', 'tensor_tensor', ['ig', 'V3'], ['rr'], out=rr[:T, :], in0=ig[:T, :], in1=Fm[:T, :], op=ALU.subtract)
                        A('pool', 'tensor_copy', ['ig'], ['grep'], out=igrep[:T, :, :], in_=ig[:T, :].unsqueeze(2).to_broadcast([T, 8, 128]))
                        A('pool', 'tensor_copy', ['nlf'], ['grep'], out=nlfrep[:T, :, :], in_=nlf[:T, :].unsqueeze(2).to_broadcast([T, 8, 128]))
                        qmT = qmk[:, 0:8, :]; kmT = qmk[:, 8:16, :]
                        for h in range(8):
                            pr, kr = ps()
                            A('pe', 'matmul', ['grep', 'cst'], [kr], out=pr[:, :T], lhsT=igrep[:T, h, :], rhs=ident[:T, :T], start=True, stop=False)
                            A('pe', 'matmul', ['grep', 'cst'], [kr], out=pr[:, :T], lhsT=nlfrep[:T, h, :], rhs=triU[:T, :T], start=False, stop=True)
                            A('dve', 'tensor_tensor', [kr, 'cst'], ['dl'], out=dl[:T, :T], in0=pr[:T, :T], in1=NEGI[:T, :T], op=ALU.add)
                            imp = V3[:, 8 + h:9 + h]
                            A('dve', 'tensor_reduce', ['dl'], [('imp', h)], out=imp[:T, :], in_=dl[:T, :T], axis=AX.X, op=ALU.max)
                            A('dve', 'tensor_scalar', [('imp', h)], ['pp0'], out=pp[:T, 0:1], in0=imp[:T, :], scalar1=-1.0, scalar2=None, op0=ALU.mult)
                            A('act', 'activation', ['dl', 'pp0'], ['ww'], out=ww[:T, :T], in_=dl[:T, :T], func=AF.Exp, bias=pp[:T, 0:1], scale=1.0)
                            pq, kq = ps()
                            A('pe', 'matmul', ['qmk'], [kq], out=pq[:T, :T], lhsT=qmT[:, h, :T], rhs=kmT[:, h, :T], start=True, stop=True)
                            A('dve', 'scalar_tensor_tensor', [kq, 'ww'], ['wi'], out=wi[:T, :T], in0=pq[:T, :T], scalar=128.0 ** -0.5, in1=ww[:T, :T],
                              op0=ALU.mult, op1=ALU.mult)
                            A('dve', 'tensor_reduce', ['wi'], ['pp1'], out=pp[:T, 1:2], in_=wi[:T, :T], axis=AX.X, op=ALU.add)
                            pT_, kT_ = ps()
                            A('pe', 'transpose', ['wi', 'cst'], [kT_], out=pT_[:T, :T], in_=wi[:T, :T], identity=ident[:T, :T])
                            A('act', 'copy', [kT_], ['wiT'], out=wiT[:T, :T], in_=pT_[:T, :T])
                            pn, kn = ps()
                            A('pe', 'matmul', ['wiT', 'vmt'], [kn], out=pn[:T, 0:256], lhsT=wiT[:T, :T], rhs=vmt[:T, h * 256:(h + 1) * 256], start=True, stop=True)
                            pc, kc_ = ps()
                            A('pe', 'matmul', ['qmk', 'Cx'], [kc_], out=pc[:T, 0:257], lhsT=qmT[:, h, :T], rhs=Cx[:, h, :], start=True, stop=True)
                            A('dve', 'tensor_tensor', ['mrep', ('imp', h)], ['pp2'], out=pp[:T, 2:3], in0=mrep[:T, h:h + 1], in1=imp[:T, :], op=ALU.max)
                            A('dve', 'tensor_tensor', ['pp2', 'V3'], [('mt', h)], out=V3[:T, h:h + 1], in0=pp[:T, 2:3], in1=Fm[:T, h:h + 1], op=ALU.add)
                            A('dve', 'tensor_tensor', ['mrep', 'pp2'], ['pp34'], out=pp[:T, 3:4], in0=mrep[:T, h:h + 1], in1=pp[:T, 2:3], op=ALU.subtract)
                            A('dve', 'tensor_tensor', [('imp', h), 'pp2'], ['pp34'], out=pp[:T, 4:5], in0=imp[:T, :], in1=pp[:T, 2:3], op=ALU.subtract)
                            A('act', 'activation', ['pp34'], ['pp34'], out=pp[:T, 3:5], in_=pp[:T, 3:5], func=AF.Exp)
                            A('act', 'activation', [kc_, 'pp34'], ['nq'], out=nq[:T, :], in_=pc[:T, 0:257], func=AF.Identity, scale=pp[:T, 3:4])
                            A('dve', 'scalar_tensor_tensor', [kn, 'pp34', 'nq'], ['hsb'], out=hsb[:T, h * 256:(h + 1) * 256], in0=pn[:T, 0:256], scalar=pp[:T, 4:5],
                              in1=nq[:T, 0:256], op0=ALU.mult, op1=ALU.add)
                            A('dve', 'scalar_tensor_tensor', ['pp1', 'pp34', 'nq'], ['pp5'], out=pp[:T, 5:6], in0=pp[:T, 1:2], scalar=pp[:T, 4:5],
                              in1=nq[:T, 256:257], op0=ALU.mult, op1=ALU.add)
                            A('act', 'activation', [('mt', h)], ['pp6'], out=pp[:T, 6:7], in_=V3[:T, h:h + 1], func=AF.Exp, scale=-1.0)
                            A('act', 'activation', ['pp5'], ['pp5'], out=pp[:T, 5:6], in_=pp[:T, 5:6], func=AF.Abs)
                            A('dve', 'tensor_tensor', ['pp5', 'pp6'], ['pp7'], out=pp[:T, 7:8], in0=pp[:T, 5:6], in1=pp[:T, 6:7], op=ALU.max)
                            A('dve', 'reciprocal', ['pp7'], ['pp7'], out=pp[:T, 7:8], in_=pp[:T, 7:8])
                            A('dve', 'tensor_scalar', ['hsb', 'pp7'], ['hsb'], out=hsb[:T, h * 256:(h + 1) * 256], in0=hsb[:T, h * 256:(h + 1) * 256],
                              scalar1=pp[:T, 7:8], scalar2=None, op0=ALU.mult)
                        A('pool', 'tensor_tensor', ['hsb'], ['osb'], out=osb[:T, :], in0=hsb[:T, :], in1=hsb[:T, :], op=ALU.mult)
                        A('dve', 'tensor_reduce', ['osb'], ['ssqb'], out=ssqb[:T, :], in_=osb[:T, :].rearrange("t (h v) -> t h v", v=256), axis=AX.X, op=ALU.add)
                        A('act', 'activation', ['ssqb', 'eps6'], ['ssqb'], out=ssqb[:T, :], in_=ssqb[:T, :], func=AF.Sqrt, bias=eps6[:T, :], scale=1.0 / 256.0)
                        A('dve', 'reciprocal', ['ssqb'], ['ssqb'], out=ssqb[:T, :], in_=ssqb[:T, :])
                        A('dve', 'tensor_tensor', ['hsb', 'ssqb'], ['hsb'], out=hsb[:T, :].rearrange("t (h v) -> t h v", v=256),
                          in0=hsb[:T, :].rearrange("t (h v) -> t h v", v=256), in1=ssqb[:T, :].unsqueeze(2).to_broadcast([T, 8, 256]), op=ALU.mult)
                        A('pool', 'tensor_tensor', ['hsb', 'mnw'], ['hsb'], out=hsb[:T, :].rearrange("t (h v) -> t h v", v=256),
                          in0=hsb[:T, :].rearrange("t (h v) -> t h v", v=256), in1=mnw[:T, :].unsqueeze(1).to_broadcast([T, 8, 256]), op=ALU.mult)
                        A('dve', 'tensor_tensor', ['hsb', 'omt'], ['ob'], out=oab[1][:T, :], in0=hsb[:T, :], in1=omt[:T, :], op=ALU.mult)
                        plb, klb = ps()
                        A('pe', 'matmul', ['sel', 'V3'] + [('mt', h) for h in range(8)] + [('imp', h) for h in range(8)], [klb],
                          out=plb[:, 0:24], lhsT=sel[:T, :], rhs=V3[:T, 0:24], start=True, stop=True)
                        A('dve', 'tensor_copy', [klb], ['lastbc'], out=lastbc[:, 0:24], in_=plb[:, 0:24])
                        mnew = lastbc[:, 0:8]; impl = lastbc[:, 8:16]; Fl = lastbc[:, 16:24]
                        abc = lastbc[:, 24:32]; bbc = lastbc[:, 32:40]
                        A('dve', 'tensor_tensor', ['lastbc'], ['bbc'], out=bbc, in0=Fl, in1=mnew, op=ALU.subtract)
                        A('dve', 'tensor_tensor', ['bbc', 'mrep'], ['abc'], out=abc, in0=bbc, in1=mrep[:, :], op=ALU.add)
                        A('dve', 'tensor_tensor', ['bbc', 'lastbc'], ['bbc'], out=bbc, in0=bbc, in1=impl, op=ALU.add)
                        A('act', 'activation', ['abc', 'bbc'], ['abc', 'bbc'], out=lastbc[:, 24:40], in_=lastbc[:, 24:40], func=AF.Exp)
                        A('dve', 'tensor_scalar', ['bbc'], ['bbc'], out=bbc, in0=bbc, scalar1=128.0 ** -0.5, scalar2=None, op0=ALU.mult)
                        A('dve', 'tensor_tensor', ['rr', 'lastbc'], ['ksc'], out=ksc[:T, :], in0=rr[:T, :], in1=impl[:T, :], op=ALU.subtract)
                        A('act', 'activation', ['ksc'], ['ksc'], out=ksc[:T, :], in_=ksc[:T, :], func=AF.Exp)
                        for h in range(8):
                            pk_, kk_ = ps()
                            A('pe', 'transpose', ['qmk', 'cst'], [kk_], out=pk_[:T, 0:128], in_=kmT[:, h, :T], identity=ident[:, :])
                            A('dve', 'tensor_scalar', [kk_, 'ksc'], ['kend'], out=kend[:T, :], in0=pk_[:T, 0:128], scalar1=ksc[:T, h:h + 1], scalar2=None, op0=ALU.mult)
                            pu, ku = ps()
                            A('pe', 'matmul', ['kend', 'vmt'], [ku], out=pu[:, 0:256], lhsT=kend[:T, :], rhs=vmt[:T, h * 256:(h + 1) * 256], start=True, stop=True)
                            A('pe', 'matmul', ['kend', 'ones'], [ku], out=pu[:, 256:257], lhsT=kend[:T, :], rhs=ones[:T, 0:1], start=True, stop=True)
                            A('pool', 'tensor_scalar', ['Cx', 'abc'], ['Cx'], out=Cx[:, h, :], in0=Cx[:, h, :], scalar1=abc[:, h:h + 1], scalar2=None, op0=ALU.mult)
                            A('dve', 'scalar_tensor_tensor', [ku, 'bbc', 'Cx'], ['Cx'], out=Cx[:, h, :], in0=pu[:, 0:257], scalar=bbc[:, h:h + 1], in1=Cx[:, h, :],
                              op0=ALU.mult, op1=ALU.add)
                        A('dve', 'tensor_copy', ['lastbc', 'abc'], ['mrep'], out=mrep[:, :], in_=mnew)
                        for bi_ in range(2):
                            transpose_to(oab[bi_], ('oa', 'ob')[bi_], T, oT, 'ctmp', 0)
                            DMA('sp', OABT_s[bi_, :, :, tok0:tok0 + T].rearrange("k p t -> p k t"), oT[:, :, :T], reads=['ctmp'])
                    DMA('sp', os_d[q].rearrange("h k v -> k h v"), S[:], reads=['S'])
                    DMA('sp', oc_d[q].rearrange("h k v -> k h v"), Cx[:, :, 0:256], reads=['Cx'])
                    A('dve', 'tensor_copy', ['Cx'], ['nq'], out=nq[:, 0:8], in_=Cx[:, :, 256])
                    pt, pk = ps()
                    A('pe', 'matmul', ['nq', 'cst'], [pk], out=pt[0:8, 0:128], lhsT=nq[:, 0:8], rhs=ident[:, :], start=True, stop=True)
                    A('act', 'copy', [pk], ['snr'], out=snr[0:8, :], in_=pt[0:8, 0:128])
                    DMA('sp', on_d[q], snr[0:8, :], reads=['snr'])
                    DMA('sp', om_d[q:q + 1, :], mrep[0:1, :], reads=['mrep'])
                    DMA('sp', oconv_d[q], ocv[0:3, :], reads=['craw'])
                P.barrier()

        def token_groups():
            groups = []
            for s in seqs:
                if s['sample'] is None:
                    for t0 in range(0, s['L'], 512):
                        groups.append((s['off'] + t0, min(512, s['L'] - t0)))
            if NS:
                groups.append((seqs[NP]['off'], NS * LS))
            return groups

        def stage3():
            with ExitStack() as st:
                g1t = bcast_load(st, "ln1g", ln1g_d, D)
                b1t = bcast_load(st, "ln1b", ln1b_d, D)
                lnt = (sb(st, "ln_stats", [128, 4, 6]), sb(st, "ln_mv", [128, 2]), sb(st, "ln_sd", [128, 2]))
                bufA = sb(st, "bufA", [128, 16, 512]); bufB = sb(st, "bufB", [128, 16, 512])
                wbs = [sb(st, f"s3w{i}", [128, 16, 256]) for i in range(2)]
                wr = Rot([0, 1])
                mixed = sb(st, "mixed", [128, 4, D])
                x1t = sb(st, "x1t", [128, D])
                gab = [sb(st, f"gab{i}", [128, 2, 256]) for i in range(2)]
                gr = Rot([0, 1])
                hsl = [sb(st, f"hsl{i}", [128, 256]) for i in range(2)]
                hr = Rot([0, 1])
                keysT = sb(st, "keysT", [128, 16, 128])
                sc = sb(st, "sc", [128, 16, 128]); sc2 = sb(st, "sc2", [128, 16, 128]); cand2 = sb(st, "cand2", [128, 8, 256])
                t16 = sb(st, "t16", [128, 16, 16]); c16 = sb(st, "c16", [128, 8, 16]); ce = sb(st, "ce", [128, 8, 16])
                sm3 = sb(st, "sm3", [128, 64])
                Zt = sm3[:, 0:8]; lnZ = sm3[:, 8:16]; b0t = sm3[:, 16:24]; the = sm3[:, 24:32]; c1t = sm3[:, 32:40]
                pgt = sb(st, "pgt", [128, 24])
                DMA('sp', sc[:, :, :], keys_d.rearrange("h n d -> n h d"), writes=['sc'])
                for g4 in range(0, 16, 4):
                    pt, pk = ps()
                    for j in range(4):
                        A('pe', 'transpose', ['sc', 'cst'], [pk], out=pt[:, j * 128:(j + 1) * 128], in_=sc[:, g4 + j, :], identity=ident[:, :])
                    A('act', 'copy', [pk], ['keysT'], out=keysT[:, g4:g4 + 4, :], in_=pt[:, :].rearrange("p (j n) -> p j n", n=128))
                wa_v = wa_d.rearrange("(kc kp) c -> kp kc c", kp=128)
                wb_v = wb_d.rearrange("(kc kp) c -> kp kc c", kp=128)
                wo_v = wo_d.rearrange("(kc kp) c -> kp kc c", kp=128)
                wq_v = wq_d.rearrange("(kc kp) c -> kp kc c", kp=128)
                for (tok0, ntok) in token_groups():
                    tiles = [(a, min(128, ntok - a)) for a in range(0, ntok, 128)]
                    DMA('sp', bufA[:, :, :ntok], OABT_s[0, :, :, tok0:tok0 + ntok].rearrange("k p t -> p k t"), writes=['bufA'])
                    DMA('sp', bufB[:, :, :ntok], OABT_s[1, :, :, tok0:tok0 + ntok].rearrange("k p t -> p k t"), writes=['bufB'])
                    for cb in range(8):
                        c0 = cb * 256
                        ia = wr.next(); ib = wr.next()
                        DMA('act', wbs[ia][:], wa_v[:, :, c0:c0 + 256], writes=[('s3w', ia)])
                        DMA('act', wbs[ib][:], wb_v[:, :, c0:c0 + 256], writes=[('s3w', ib)])
                        for ti, (a, T) in enumerate(tiles):
                            gi = gr.next(); gt = gab[gi]
                            DMA('sp', gt[:T, 0, :], PTM_s[tok0 + a:tok0 + a + T, 6144 + c0:6144 + c0 + 256], writes=[('gab', gi)])
                            DMA('sp', gt[:T, 1, :], PTM_s[tok0 + a:tok0 + a + T, 8192 + c0:8192 + c0 + 256], writes=[('gab', gi)])
                            pa, ka = ps()
                            for kc in range(16):
                                A('pe', 'matmul', ['bufA', ('s3w', ia)], [ka], out=pa[:T, 0:256], lhsT=bufA[:, kc, a:a + T], rhs=wbs[ia][:, kc, :],
                                  start=(kc == 0), stop=(kc == 15))
                            pb_, kb_ = ps()
                            for kc in range(16):
                                A('pe', 'matmul', ['bufB', ('s3w', ib)], [kb_], out=pb_[:T, 0:256], lhsT=bufB[:, kc, a:a + T], rhs=wbs[ib][:, kc, :],
                                  start=(kc == 0), stop=(kc == 15))
                            A('dve', 'tensor_tensor', [ka, ('gab', gi)], [('mixed', ti)], out=mixed[:T, ti, c0:c0 + 256], in0=pa[:T, 0:256], in1=gt[:T, 0, :], op=ALU.mult)
                            A('dve', 'tensor_tensor', [kb_, ('gab', gi)], [('gab', gi)], out=gt[:T, 1, :], in0=pb_[:T, 0:256], in1=gt[:T, 1, :], op=ALU.mult)
                            A('pool', 'tensor_tensor', [('mixed', ti), ('gab', gi)], [('mixed', ti)], out=mixed[:T, ti, c0:c0 + 256], in0=mixed[:T, ti, c0:c0 + 256],
                              in1=gt[:T, 1, :], op=ALU.add)
                    for ti, (a, T) in enumerate(tiles):
                        transpose_to(mixed[:, ti, :], ('mixed', ti), T, bufA, 'bufA', a)
                    for cb in range(8):
                        c0 = cb * 256
                        iw = wr.next()
                        DMA('act', wbs[iw][:], wo_v[:, :, c0:c0 + 256], writes=[('s3w', iw)])
                        for ti, (a, T) in enumerate(tiles):
                            hi = hr.next()
                            DMA('sp', hsl[hi][:T, :], H_s[tok0 + a:tok0 + a + T, c0:c0 + 256], writes=[('hsl', hi)])
                            pa, ka = ps()
                            for kc in range(16):
                                A('pe', 'matmul', ['bufA', ('s3w', iw)], [ka], out=pa[:T, 0:256], lhsT=bufA[:, kc, a:a + T], rhs=wbs[iw][:, kc, :],
                                  start=(kc == 0), stop=(kc == 15))
                            A('dve', 'scalar_tensor_tensor', [('hsl', hi), ka], [('mixed', ti)], out=mixed[:T, ti, c0:c0 + 256], in0=hsl[hi][:T, :], scalar=ALPHA,
                              in1=pa[:T, 0:256], op0=ALU.mult, op1=ALU.add)
                    for ti, (a, T) in enumerate(tiles):
                        layer_norm(lnt, mixed[:, ti, :], ('mixed', ti), T, g1t, "ln1g", b1t, "ln1b", x1t, 'x1t')
                        DMA('sp', X1_s[tok0 + a:tok0 + a + T, :], x1t[:T, :], reads=['x1t'])
                        transpose_to(x1t, 'x1t', T, bufB, 'bufB', a)
                    DMA('sp', X1T_s[:, :, tok0:tok0 + ntok].rearrange("k p t -> p k t"), bufB[:, :, :ntok], reads=['bufB'])
                    for blk in range(16):
                        iw = wr.next()
                        DMA('act', wbs[iw][:, :, 0:128], wq_v[:, :, blk * 128:(blk + 1) * 128], writes=[('s3w', iw)])
                        pa, ka = ps()
                        for kc in range(16):
                            A('pe', 'matmul', ['bufB', ('s3w', iw)], [ka], out=pa[:, :ntok], lhsT=wbs[iw][:, kc, 0:128], rhs=bufB[:, kc, :ntok],
                              start=(kc == 0), stop=(kc == 15))
                        A('act', 'copy', [ka], ['bufA'], out=bufA[:, blk, :ntok], in_=pa[:, :ntok])
                    for ti, (a, T) in enumerate(tiles):
                        for g4 in range(0, 16, 4):
                            pt, pk = ps()
                            for j in range(4):
                                A('pe', 'matmul', ['bufA', 'keysT'], [pk], out=pt[:T, j * 128:(j + 1) * 128], lhsT=bufA[:, g4 + j, a:a + T], rhs=keysT[:, g4 + j, :],
                                  start=True, stop=True)
                            A('act', 'copy', [pk], ['sc'], out=sc[:T, g4:g4 + 4, :], in_=pt[:T, :].rearrange("t (j n) -> t j n", n=128))
                        for hp in range(16):
                            A('dve', 'max', ['sc'], ['t16'], out=t16[:T, hp, 0:8], in_=sc[:T, hp, :])
                            A('dve', 'match_replace', ['sc', 't16'], ['sc2'], out=sc2[:T, hp, :], in_to_replace=t16[:T, hp, 0:8], in_values=sc[:T, hp, :], imm_value=-1e30)
                            A('dve', 'max', ['sc2'], ['t16'], out=t16[:T, hp, 8:16], in_=sc2[:T, hp, :])
                        t16v = t16[:, :, :].rearrange("t (h p) k -> t h p k", p=2)
                        cand = sc2[:, :, :].rearrange("t (h x) n -> t h (x n)", x=2)
                        A('dve', 'tensor_tensor', ['t16'], ['sc2'], out=cand[:T, :, :].rearrange("t h (i j) -> t h i j", j=16),
                          in0=t16v[:T, :, 0, :].unsqueeze(3).to_broadcast([T, 8, 16, 16]), in1=t16v[:T, :, 1, :].unsqueeze(2).to_broadcast([T, 8, 16, 16]), op=ALU.add)
                        for h in range(8):
                            A('dve', 'max', ['sc2'], ['c16'], out=c16[:T, h, 0:8], in_=cand[:T, h, :])
                            A('dve', 'match_replace', ['sc2', 'c16'], ['cand2'], out=cand2[:T, h, :], in_to_replace=c16[:T, h, 0:8], in_values=cand[:T, h, :], imm_value=-1e30)
                            A('dve', 'max', ['cand2'], ['c16'], out=c16[:T, h, 8:16], in_=cand2[:T, h, :])
                        A('dve', 'tensor_tensor', ['c16'], ['ce'], out=ce[:T, :, :], in0=c16[:T, :, :], in1=c16[:T, :, 0:1].to_broadcast([T, 8, 16]), op=ALU.subtract)
                        A('act', 'activation', ['ce'], ['ce'], out=ce[:T, :, :], in_=ce[:T, :, :], func=AF.Exp)
                        A('dve', 'tensor_reduce', ['ce'], ['Zt'], out=Zt[:T, :], in_=ce[:T, :, :], axis=AX.X, op=ALU.add)
                        A('act', 'activation', ['Zt'], ['lnZ'], out=lnZ[:T, :], in_=Zt[:T, :], func=AF.Ln)
                        A('dve', 'tensor_copy', ['c16'], ['pgt'], out=pgt[:T, 0:8], in_=c16[:T, :, 15])
                        A('dve', 'tensor_tensor', ['t16', 'lnZ'], ['pgt'], out=pgt[:T, 8:16], in0=t16v[:T, :, 0, 0], in1=lnZ[:T, :], op=ALU.add)
                        A('dve', 'tensor_copy', ['t16'], ['pgt'], out=pgt[:T, 16:24], in_=t16v[:T, :, 1, 0])
                        scv = sc[:, :, :].rearrange("t (h p) n -> t h p n", p=2)
                        DMA('sp', PG_s[tok0 + a:tok0 + a + T, 0, :].rearrange("t (h n) -> t h n", n=128), scv[:T, :, 0, :], reads=['sc'])
                        DMA('sp', PG_s[tok0 + a:tok0 + a + T, 1, :].rearrange("t (h n) -> t h n", n=128), scv[:T, :, 1, :], reads=['sc'])
                        DMA('sp', PGT_s[tok0 + a:tok0 + a + T, :], pgt[:T, :], reads=['pgt'])
                P.barrier()

        def stage45():
            with ExitStack() as st:
                bufX16 = sb16(st, "bufX16", [128, 16, 512])
                acc = sb(st, "acc", [128, 4, D])
                wg_v = wg_d.rearrange("(kc kp) c -> kp kc c", kp=128)
                wp_v = wp_d.rearrange("(kc kp) c -> kp kc c", kp=128)
                for (tok0, ntok) in token_groups():
                    tiles = [(a, min(128, ntok - a)) for a in range(0, ntok, 128)]
                    DMA('pool', bufX16[:, :, :ntok], X1T_s[:, :, tok0:tok0 + ntok].rearrange("k p t -> p k t"), writes=['bufX16'])
                    with ExitStack() as st2:
                        nt = len(tiles)
                        gs = [sb(st2, f"gs{i}", [128, 2, 1024]) for i in range(nt)]
                        gE = [sb16(st2, f"gE{i}", [128, 2, 1024]) for i in range(nt)]
                        gth = [sb(st2, f"gth{i}", [128, 24]) for i in range(nt)]
                        uraws = [sb(st2, f"uraw{i}", [128, D]) for i in range(2)]; ur = Rot([0, 1])
                        UTs = [sb16(st2, f"UT{i}", [128, 16, 128]) for i in range(2)]; utr = Rot([0, 1])
                        vsts = [sb(st2, f"vst{i}", [128, D]) for i in range(2)]; vsr = Rot([0, 1])
                        gEf = vsts[0][:, :].rearrange("p (x n) -> p x n", x=2)
                        vt16 = [[sb16(st2, f"vt16_{g}_{i}", [128, D]) for i in range(4)] for g in range(2)]
                        gaT = [sb16(st2, f"gaT{g}", [128, 4, 512]) for g in range(2)]
                        NW = 3
                        w1s = [sb16(st2, f"w1{i}", [128, 1024]) for i in range(NW)]
                        w2s = [sb16(st2, f"w2{i}", [128, 1024]) for i in range(NW)]
                        w3s = [sb16(st2, f"w3{i}", [128, 1024]) for i in range(NW)]
                        ntaus = [sb(st2, f"ntau{i}", [128, 8]) for i in range(NW)]
                        Gds = [sb(st2, f"Gd{i}", [128, 4, 128]) for i in range(2)]
                        coef16 = [sb16(st2, f"coef{i}", [128, 4, 128]) for i in range(2)]
                        for ti, (a, T) in enumerate(tiles):
                            A('pool', 'memset', [], [('acc', ti)], ap=acc[:T, ti, :], constant=0.0)
                            DMA('sp', gs[ti][:T, :, :], PG_s[tok0 + a:tok0 + a + T, :, :], writes=[('gs', ti)])
                            DMA('sp', gth[ti][:T, :], PGT_s[tok0 + a:tok0 + a + T, :], writes=[('gth', ti)])
                            for half in range(2):
                                A('dve', 'tensor_tensor', [('gs', ti), ('gth', ti)], [('vst', 0)], out=gEf[:T, half, :].rearrange("t (h n) -> t h n", n=128),
                                  in0=gs[ti][:T, half, :].rearrange("t (h n) -> t h n", n=128),
                                  in1=gth[ti][:T, 8 + 8 * half:16 + 8 * half].unsqueeze(2).to_broadcast([T, 8, 128]), op=ALU.subtract)
                            A('act', 'activation', [('vst', 0)], [('vst', 0)], out=gEf[:T, :, :], in_=gEf[:T, :, :], func=AF.Exp)
                            A('dve', 'tensor_copy', [('vst', 0)], [('gE', ti)], out=gE[ti][:T, 0, :], in_=gEf[:T, 0, :])
                            A('dve', 'tensor_scalar', [('vst', 0)], [('gE', ti)], out=gE[ti][:T, 1, :], in0=gEf[:T, 1, :], scalar1=0.5, scalar2=None, op0=ALU.mult)

                        def u_side(ag, j):
                            g = ag % 2
                            ab = ag * 4 + j
                            iu = ur.next(); uraw = uraws[iu]
                            DMA('sp', uraw[:, :], pu_d[ab * 128:(ab + 1) * 128, :], writes=[('uraw', iu)])
                            isg = vsr.next()
                            DMA('sp', vsts[isg][:, :], pv_d[ab * 128:(ab + 1) * 128, :], writes=[('vst', isg)])
                            A('act', 'copy', [('vst', isg)], [('vt16', g, j)], out=vt16[g][j][:, :], in_=vsts[isg][:, :])
                            iut = utr.next(); UT = UTs[iut]
                            for g4 in range(0, 16, 4):
                                pt, pk = ps()
                                for jj in range(4):
                                    kc = g4 + jj
                                    A('pe', 'transpose', [('uraw', iu), 'cst'], [pk], out=pt[:, jj * 128:(jj + 1) * 128], in_=uraw[:, kc * 128:(kc + 1) * 128], identity=ident[:, :])
                                A('dve' if g4 % 8 else 'act', 'tensor_copy' if g4 % 8 else 'copy', [pk], [('UT', iut)], out=UT[:, g4:g4 + 4, :],
                                  in_=pt[:, :].rearrange("p (j e) -> p j e", e=128))
                            pA, kA = ps()
                            for kc in range(16):
                                A('pe', 'matmul', [('UT', iut), 'bufX16'], [kA], out=pA[:, :ntok], lhsT=UT[:, kc, :], rhs=bufX16[:, kc, :ntok], start=(kc == 0), stop=(kc == 15))
                            A('act', 'activation', [kA], [('gaT', g, j)], out=gaT[g][:, j, :ntok], in_=pA[:, :ntok], func=AF.Gelu)

                        items = [(ag, ti, j) for ag in range(32) for ti in range(nt) for j in range(4)]
                        jsplit = [[j for j in range(4) if (j * nt) // 4 == ti] for ti in range(nt)]

                        def phaseA0(k):
                            ag, ti, j = items[k]; a, T = tiles[ti]; b = k % NW
                            s0v = gs[ti][:T, 0, :].rearrange("t (h n) -> t h n", n=128)
                            ab = ag * 4 + j
                            A('dve', 'scalar_tensor_tensor', [('gs', ti), ('gth', ti)], [('ntau', b)], out=ntaus[b][:T, :], in0=s0v[:, :, ab], scalar=DELTA,
                              in1=gth[ti][:T, 0:8], op0=ALU.add, op1=ALU.subtract)

                        def phaseA(k):
                            ag, ti, j = items[k]; a, T = tiles[ti]; b = k % NW
                            s1v = gs[ti][:T, 1, :].rearrange("t (h n) -> t h n", n=128)
                            w1v = w1s[b][:T, :].rearrange("t (h n) -> t h n", n=128)
                            for h in range(8):
                                A('act', 'activation', [('gs', ti), ('ntau', b)], [('w1', b)], out=w1v[:, h, :], in_=s1v[:, h, :], func=AF.Sign,
                                  bias=ntaus[b][:T, h:h + 1], scale=1.0)

                        def phaseBC(k):
                            ag, ti, j = items[k]; a, T = tiles[ti]; b = k % NW
                            ab = ag * 4 + j
                            e0v = gE[ti][:T, 0, :].rearrange("t (h n) -> t h n", n=128)
                            A('dve', 'scalar_tensor_tensor', [('w1', b), ('gE', ti)], [('w2', b)], out=w2s[b][:T, :], in0=w1s[b][:T, :], scalar=1.0, in1=gE[ti][:T, 1, :],
                              op0=ALU.add, op1=ALU.mult)
                            A('pool', 'tensor_tensor', [('w2', b), ('gE', ti)], [('w3', b)], out=w3s[b][:T, :].rearrange("t (h n) -> t h n", n=128),
                              in0=w2s[b][:T, :].rearrange("t (h n) -> t h n", n=128), in1=e0v[:, :, ab:ab + 1].to_broadcast([T, 8, 128]), op=ALU.mult)

                        def phaseD(k):
                            ag, ti, j = items[k]; a, T = tiles[ti]; b = k % NW
                            g = ag % 2
                            c = (ag * nt + ti) % 2
                            A('dve', 'tensor_reduce', [('w3', b)], [('Gd', c)], out=Gds[c][:T, j, :], in_=w3s[b][:T, :].rearrange("t (h n) -> t n h", n=128), axis=AX.X, op=ALU.add)
                            if j < 3:
                                return
                            pT_, kT_ = ps()
                            for jj in range(4):
                                A('pe', 'transpose', [('Gd', c), 'cst'], [kT_], out=pT_[:, jj * 128:jj * 128 + T], in_=Gds[c][:T, jj, :], identity=ident[:T, :T])
                            A('dve', 'tensor_tensor', [kT_] + [('gaT', g, jj) for jj in range(4)], [('coef', c)], out=coef16[c][:, :, :T],
                              in0=pT_[:, :].rearrange("p (j t) -> p j t", t=128)[:, :, :T], in1=gaT[g][:, :, a:a + T], op=ALU.mult)
                            for cbk in range(4):
                                po, ko = ps()
                                for jj in range(4):
                                    A('pe', 'matmul', [('coef', c), ('vt16', g, jj)], [ko], out=po[:T, :], lhsT=coef16[c][:, jj, :T],
                                      rhs=vt16[g][jj][:, cbk * 512:(cbk + 1) * 512], start=(jj == 0), stop=(jj == 3))
                                A('dve', 'tensor_tensor', [ko, ('acc', ti)], [('acc', ti)], out=acc[:T, ti, cbk * 512:(cbk + 1) * 512],
                                  in0=po[:T, :], in1=acc[:T, ti, cbk * 512:(cbk + 1) * 512], op=ALU.add)

                        for j in range(4):
                            u_side(0, j)
                        nitems = len(items)
                        LAGD = 3
                        phaseA0(0)
                        for k in range(nitems + LAGD):
                            if 0 <= k - LAGD < nitems:
                                ag, ti, j = items[k - LAGD]
                                if j == 0 and ag + 1 < 32:
                                    for jn in jsplit[ti]:
                                        u_side(ag + 1, jn)
                            if k + 1 < nitems:
                                phaseA0(k + 1)
                            if k < nitems:
                                phaseA(k)
                            if 0 <= k - 1 < nitems:
                                phaseBC(k - 1)
                            if 0 <= k - LAGD < nitems:
                                phaseD(k - LAGD)
                    P.barrier()
                    with ExitStack() as st3:
                        g2t = bcast_load(st3, "ln2g", ln2g_d, D)
                        b2t = bcast_load(st3, "ln2b", ln2b_d, D)
                        lnt = (sb(st3, "ln_stats", [128, 4, 6]), sb(st3, "ln_mv", [128, 2]), sb(st3, "ln_sd", [128, 2]))
                        x1t = sb(st3, "x1t5", [128, D]); r2 = sb(st3, "r2", [128, D])
                        bufX = sb(st3, "bufX5", [128, 16, 512])
                        wgs = [sb(st3, f"wg{i}", [128, 16, 256]) for i in range(2)]; wgr = Rot([0, 1])
                        wps = [sb(st3, f"wp{i}", [128, 2, 256]) for i in range(2)]
                        ptl = sb(st3, "ptl", [128, 256]); pT = sb(st3, "pT", [128, 2, 512])
                        gsb = [sb(st3, f"gsb{i}", [128, 256]) for i in range(2)]; yb = [sb(st3, f"yb{i}", [128, 256]) for i in range(2)]
                        gr5 = Rot([0, 1])
                        for ti, (a, T) in enumerate(tiles):
                            DMA('sp', x1t[:T, :], X1_s[tok0 + a:tok0 + a + T, :], writes=['x1t5'])
                            A('dve', 'scalar_tensor_tensor', ['x1t5', ('acc', ti)], ['r2'], out=r2[:T, :], in0=x1t[:T, :], scalar=ALPHA, in1=acc[:T, ti, :],
                              op0=ALU.mult, op1=ALU.add)
                            layer_norm(lnt, r2, 'r2', T, g2t, "ln2g", b2t, "ln2b", acc[:, ti, :], ('acc', ti))
                            transpose_to(acc[:, ti, :], ('acc', ti), T, bufX, 'bufX', a)
                            DMA('sp', ptl[:T, :], p_d[tok0 + a:tok0 + a + T, :], writes=['ptl'])
                            transpose_to(ptl, 'ptl', T, pT, 'pT', a, nkc=2)
                        for cb in range(8):
                            c0 = cb * 256
                            iw = wgr.next()
                            DMA('act', wgs[iw][:], wg_v[:, :, c0:c0 + 256], writes=[('wg', iw)])
                            DMA('act', wps[iw][:], wp_v[:, :, c0:c0 + 256], writes=[('wp', iw)])
                            for ti, (a, T) in enumerate(tiles):
                                pg_, kg_ = ps()
                                for kc in range(16):
                                    A('pe', 'matmul', ['bufX', ('wg', iw)], [kg_], out=pg_[:T, 0:256], lhsT=bufX[:, kc, a:a + T], rhs=wgs[iw][:, kc, :],
                                      start=(kc == 0), stop=(kc == 15))
                                ig_ = gr5.next()
                                A('act', 'activation', [kg_], [('gsb', ig_)], out=gsb[ig_][:T, :], in_=pg_[:T, 0:256], func=AF.Sigmoid)
                                pp_, kp_ = ps()
                                for kc in range(2):
                                    A('pe', 'matmul', ['pT', ('wp', iw)], [kp_], out=pp_[:T, 0:256], lhsT=pT[:, kc, a:a + T], rhs=wps[iw][:, kc, :],
                                      start=(kc == 0), stop=(kc == 1))
                                A('dve', 'tensor_tensor', [kp_, ('gsb', ig_)], [('yb', ig_)], out=yb[ig_][:T, :], in0=pp_[:T, 0:256], in1=gsb[ig_][:T, :], op=ALU.mult)
                                A('pool', 'tensor_tensor', [('yb', ig_), ('acc', ti)], [('yb', ig_)], out=yb[ig_][:T, :], in0=yb[ig_][:T, :], in1=acc[:T, ti, c0:c0 + 256],
                                  op=ALU.add)
                                DMA('sp', y_d[tok0 + a:tok0 + a + T, c0:c0 + 256], yb[ig_][:T, :], reads=[('yb', ig_)])
                    P.barrier()

        stage1()
        if debug_stage is None or debug_stage >= 2:
            stage2()
        if debug_stage is None or debug_stage >= 3:
            stage3()
        if debug_stage is None or debug_stage >= 4:
            stage45()
        P.emit(top)
    return nc


def make_consts():
    c = np.zeros((128, 640), np.float32)
    i = np.arange(128)
    c[:, 0:128] = np.eye(128, dtype=np.float32)
    c[:, 128:256] = (i[:, None] <= i[None, :]).astype(np.float32)
    c[:, 256:384] = np.where(i[None, :] > i[:, None], NEG, 0.0)
    c[:, 384:512] = np.where(i[None, :] >= i[:, None], NEG, 0.0)
    c[:, 512:640] = np.where(i[None, :] < i[:, None], NEG, 0.0)
    return c


_NC_CACHE = {}


def kernel(x_prompt, x_sample, state_gdn_conv, state_gdn_s, state_mlstm_c, state_mlstm_n, state_mlstm_m,
           p_prompt, p_sample, ln0_g, ln0_b, w_in, gdn_conv_w, gdn_a_log, gdn_dt_bias, gdn_norm_w,
           mlstm_b_i, mlstm_b_f, mlstm_norm_w, w_branch_a, w_branch_b, w_out, ln1_g, ln1_b,
           peer_wq, peer_keys, peer_u, peer_v, ln2_g, ln2_b, ple_proj, ple_gate):
    NCORES = 8
    f = np.float32
    x_prompt = np.asarray(x_prompt, f); x_sample = np.asarray(x_sample, f)
    B, LP, _ = x_prompt.shape
    BS, LS, _ = x_sample.shape
    NP = B // NCORES
    NS = BS // NCORES
    key = (NP, LP, NS, LS)
    if key not in _NC_CACHE:
        _NC_CACHE[key] = build(NP, LP, NS, LS)
    nc = _NC_CACHE[key]

    def c(a):
        return np.ascontiguousarray(np.asarray(a, f))

    shared = {
        "ln0_g": c(ln0_g), "ln0_b": c(ln0_b), "w_in": c(np.asarray(w_in)[0]), "gdn_conv_w": c(np.asarray(gdn_conv_w)[0]),
        "gdn_a_log": c(np.asarray(gdn_a_log)[0]), "gdn_dt_bias": c(np.asarray(gdn_dt_bias)[0]), "gdn_norm_w": c(np.asarray(gdn_norm_w)[0]),
        "mlstm_b_i": c(np.asarray(mlstm_b_i)[0]), "mlstm_b_f": c(np.asarray(mlstm_b_f)[0]), "mlstm_norm_w": c(np.asarray(mlstm_norm_w)[0]),
        "w_branch_a": c(np.asarray(w_branch_a)[0]), "w_branch_b": c(np.asarray(w_branch_b)[0]), "w_out": c(np.asarray(w_out)[0]),
        "ln1_g": c(np.asarray(ln1_g)[0]), "ln1_b": c(np.asarray(ln1_b)[0]), "peer_wq": c(np.asarray(peer_wq)[0]),
        "peer_keys": c(np.asarray(peer_keys)[0]).reshape(16, 128, 128), "peer_u": c(np.asarray(peer_u)[0]), "peer_v": c(np.asarray(peer_v)[0]),
        "ln2_g": c(np.asarray(ln2_g)[0]), "ln2_b": c(np.asarray(ln2_b)[0]), "ple_proj": c(np.asarray(ple_proj)[0]), "ple_gate": c(np.asarray(ple_gate)[0]),
        "consts": make_consts(),
    }
    pp = np.asarray(p_prompt, f)[0]; psm = np.asarray(p_sample, f)[0]
    sconv = np.asarray(state_gdn_conv, f)[0]; ssv = np.asarray(state_gdn_s, f)[0]
    scv = np.asarray(state_mlstm_c, f)[0]; snv = np.asarray(state_mlstm_n, f)[0]; smv = np.asarray(state_mlstm_m, f)[0]
    in_maps = []
    for ci in range(NCORES):
        pi = slice(ci * NP, (ci + 1) * NP); si = slice(ci * NS, (ci + 1) * NS)
        m = dict(shared)
        m["x"] = np.ascontiguousarray(np.concatenate([x_prompt[pi].reshape(NP * LP, D), x_sample[si].reshape(NS * LS, D)], 0))
        m["p"] = np.ascontiguousarray(np.concatenate([pp[pi].reshape(NP * LP, 256), psm[si].reshape(NS * LS, 256)], 0))
        m["sconv"] = c(sconv[si]); m["ss"] = c(ssv[si]); m["sc"] = c(scv[si]); m["sn"] = c(snv[si]); m["sm"] = c(smv[si])
        in_maps.append(m)
    res = run_bass_kernel_spmd(nc, in_maps, core_ids=list(range(NCORES)))
    rs = res.results
    y_p = np.empty((B, LP, D), f); y_s = np.empty((BS, LS, D), f)
    names = [("oconv", (3, 6144)), ("os", (16, 128, 128)), ("oc", (8, 128, 256)), ("on", (8, 128)), ("om", (8,))]
    pst = [np.empty((1, B) + shp, f) for _, shp in names]
    sst = [np.empty((1, BS) + shp, f) for _, shp in names]
    for ci in range(NCORES):
        r = rs[ci]
        y = np.asarray(r["y"])
        y_p[ci * NP:(ci + 1) * NP] = y[:NP * LP].reshape(NP, LP, D)
        y_s[ci * NS:(ci + 1) * NS] = y[NP * LP:].reshape(NS, LS, D)
        for k, (nm, shp) in enumerate(names):
            a = np.asarray(r[nm])
            pst[k][0, ci * NP:(ci + 1) * NP] = a[:NP]
            sst[k][0, ci * NS:(ci + 1) * NS] = a[NP:]
    return (y_p, y_s, *pst, *sst)
```

```python
import numpy as np
from contextlib import ExitStack
import concourse.bass as bass
import concourse.mybir as mybir
from concourse.bass_utils import run_bass_kernel_spmd

F32 = mybir.dt.float32
BF16 = mybir.dt.bfloat16
DELTA = 1e-5
AF = mybir.ActivationFunctionType
ALU = mybir.AluOpType
AX = mybir.AxisListType

D = 2048
NIN = 18480
HA = 16
HB = 8
ALPHA = 2.0 ** 0.25
LN_EPS = 1e-5
RMS_EPS = 1e-6
NEG = -30000.0
NPTM = 10288
ENGS = ('pe', 'dve', 'act', 'pool', 'sp')
KRING = 12
GEN = 30000


class Prog:
    def __init__(self, nc):
        self.nc = nc
        self.ops = []
        self.lastw = {}
        self.readers = {}
        self.last_eng = {}
        self.dmas = []

    max_ops = None

    def add(self, eng, meth, kw, reads=(), writes=(), dma=False):
        idx = len(self.ops)
        if Prog.max_ops is not None and idx >= Prog.max_ops:
            return idx
        psr = [k for k in reads if isinstance(k, tuple) and k[0] == 'ps']
        if psr:
            reads = [k for k in reads if k not in psr]
            writes = list(writes) + psr
        deps = set()
        for k in reads:
            w = self.lastw.get(k)
            if w is not None:
                deps.add(w)
        for k in writes:
            w = self.lastw.get(k)
            if w is not None:
                deps.add(w)
            for r in self.readers.get(k, ()):
                deps.add(r)
        if eng == 'pe' and not dma:
            deps = {d for d in deps if not (self.ops[d][0] == 'pe' and not self.ops[d][4])}
        self.ops.append([eng, (meth, kw), deps, False, dma, None])
        for k in reads:
            self.readers.setdefault(k, []).append(idx)
        for k in writes:
            self.lastw[k] = idx
            self.readers[k] = []
        if dma:
            self.dmas.append(idx)
        else:
            self.last_eng[eng] = idx
        return idx

    def barrier(self):
        deps = set(self.last_eng.values()) | set(self.dmas)
        for e in ENGS:
            self.ops.append([e, None, set(deps), False, False, None])
        self.dmas = []
        self.lastw = {}
        self.readers = {}

    def emit(self, st):
        nc = self.nc
        eo = {'pe': nc.tensor, 'dve': nc.vector, 'act': nc.scalar, 'pool': nc.gpsimd, 'sp': nc.sync}
        ops = self.ops
        for op in ops:
            for d in op[2]:
                ops[d][3] = True
        nsig = {e: 0 for e in ENGS}
        for op in ops:
            if op[3] and not op[4]:
                nsig[op[0]] += 1
        sems = []
        csem = {}
        for e in ENGS:
            csem[e] = []
            for g in range(nsig[e] // GEN + 1):
                s = st.enter_context(nc.semaphore(f"c_{e}_{g}"))
                csem[e].append(len(sems))
                sems.append(s)
        dsem = {}
        for e in ('sp', 'act', 'pool'):
            dsem[e] = []
            for i in range(KRING):
                s = st.enter_context(nc.semaphore(f"d_{e}_{i}"))
                dsem[e].append(len(sems))
                sems.append(s)
        cnt = {e: 0 for e in ENGS}
        dcnt = {e: 0 for e in ENGS}
        known = {e: {} for e in ENGS}
        for op in ops:
            eng, fn, deps, sig, dma, _ = op
            e = eo[eng]
            need = {}
            for d in deps:
                s, v = ops[d][5]
                if need.get(s, 0) < v:
                    need[s] = v
            if dma:
                n = dcnt[eng]
                slot = n % KRING
                val = 16 * (n // KRING + 1)
                s = dsem[eng][slot]
                if n >= KRING and need.get(s, 0) < val - 16:
                    need[s] = val - 16
                dcnt[eng] += 1
                op[5] = (s, val)
            kn = known[eng]
            for s, v in need.items():
                if kn.get(s, 0) < v:
                    e.wait_ge(sems[s], v)
                    kn[s] = v
            if fn is None:
                continue
            ins = getattr(e, fn[0])(**fn[1])
            if dma:
                ins.then_inc(sems[op[5][0]], 16)
            elif sig:
                c = cnt[eng]
                cnt[eng] += 1
                g = c // GEN
                op[5] = (csem[eng][g], c % GEN + 1)
                ins.then_inc(sems[csem[eng][g]], 1)
        for q in ('sp', 'act', 'pool'):
            n = dcnt[q]
            for slot in range(min(n, KRING)):
                last = ((n - 1 - slot) // KRING) * KRING + slot
                val = 16 * (last // KRING + 1)
                s = dsem[q][slot]
                if known['sp'].get(s, 0) < val:
                    nc.sync.wait_ge(sems[s], val)
                    known['sp'][s] = val


class Rot:
    def __init__(self, items):
        self.items = items
        self.i = 0

    def next(self):
        it = self.items[self.i % len(self.items)]
        self.i += 1
        return it


def build(NP, LP, NS, LS, debug_stage=None):
    nc = bass.Bass("TRN2", target_bir_lowering=False)
    P = Prog(nc)
    NQ = NP + NS
    LTOT = NP * LP + NS * LS
    seqs = []
    off = 0
    for i in range(NP):
        seqs.append(dict(off=off, L=LP, T=min(128, LP), sample=None, q=i))
        off += LP
    for i in range(NS):
        seqs.append(dict(off=off, L=LS, T=LS, sample=i, q=NP + i))
        off += LS
    for s in seqs:
        s['cb'] = s['off'] + 3 * s['q']
    NCOLS = LTOT + 3 * NQ

    def din(name, shape):
        return nc.dram_tensor(name, list(shape), F32, kind="ExternalInput").ap()

    def dout(name, shape):
        return nc.dram_tensor(name, list(shape), F32, kind="ExternalOutput").ap()

    def dscr(name, shape, dbg=False):
        kind = "ExternalOutput" if dbg else "Internal"
        return nc.dram_tensor(name, list(shape), F32, kind=kind).ap()

    x_d = din("x", [LTOT, D])
    p_d = din("p", [LTOT, 256])
    NSS = max(NS, 1)
    sconv_d = din("sconv", [NSS, 3, 6144])
    ss_d = din("ss", [NSS, 16, 128, 128])
    sc_d = din("sc", [NSS, 8, 128, 256])
    sn_d = din("sn", [NSS, 8, 128])
    sm_d = din("sm", [NSS, 8])
    ln0g_d = din("ln0_g", [D]); ln0b_d = din("ln0_b", [D])
    win_d = din("w_in", [D, NIN])
    convw_d = din("gdn_conv_w", [4, 6144])
    alog_d = din("gdn_a_log", [16]); dtb_d = din("gdn_dt_bias", [16])
    gnw_d = din("gdn_norm_w", [128])
    bi_d = din("mlstm_b_i", [8]); bf_d = din("mlstm_b_f", [8])
    mnw_d = din("mlstm_norm_w", [256])
    wa_d = din("w_branch_a", [D, D]); wb_d = din("w_branch_b", [D, D]); wo_d = din("w_out", [D, D])
    ln1g_d = din("ln1_g", [D]); ln1b_d = din("ln1_b", [D])
    wq_d = din("peer_wq", [D, D])
    keys_d = din("peer_keys", [16, 128, 128])
    pu_d = din("peer_u", [16384, D]); pv_d = din("peer_v", [16384, D])
    ln2g_d = din("ln2_g", [D]); ln2b_d = din("ln2_b", [D])
    wp_d = din("ple_proj", [256, D]); wg_d = din("ple_gate", [D, D])
    cst_d = din("consts", [128, 5 * 128])

    y_d = dout("y", [LTOT, D])
    oconv_d = dout("oconv", [NQ, 3, 6144])
    os_d = dout("os", [NQ, 16, 128, 128])
    oc_d = dout("oc", [NQ, 8, 128, 256])
    on_d = dout("on", [NQ, 8, 128])
    om_d = dout("om", [NQ, 8])

    dbg = debug_stage is not None
    H_s = dscr("H_s", [LTOT, D], dbg)
    QKVT_s = dscr("QKVT_s", [8192, NCOLS], dbg)
    PTM_s = dscr("PTM_s", [LTOT, NPTM], dbg)
    OABT_s = dscr("OABT_s", [2, 16, 128, LTOT], dbg)
    X1_s = dscr("X1_s", [LTOT, D], dbg)
    X1T_s = dscr("X1T_s", [16, 128, LTOT], dbg)
    PG_s = dscr("PG_s", [LTOT, 2, 1024], dbg)
    PGT_s = dscr("PGT_s", [LTOT, 24], dbg)
    UT16_s = nc.dram_tensor("UT16_s", [128, 128, 16 * 128], BF16, kind="Internal").ap()
    V16_s = nc.dram_tensor("V16_s", [16384, D], BF16, kind="Internal").ap()

    with ExitStack() as top:
        uniq = [0]

        def sb(st, name, shape):
            uniq[0] += 1
            return st.enter_context(nc.sbuf_tensor(f"{name}_{uniq[0]}", list(shape), F32))

        def sb16(st, name, shape):
            uniq[0] += 1
            return st.enter_context(nc.sbuf_tensor(f"{name}_{uniq[0]}", list(shape), BF16))

        psb = [top.enter_context(nc.psum_tensor(f"ps{i}", [128, 512], F32)) for i in range(8)]
        psrot = Rot(list(range(8)))

        def ps():
            i = psrot.next()
            return psb[i], ('ps', i)

        def A(eng, meth, reads, writes, **kw):
            return P.add(eng, meth, kw, reads, writes)

        def DMA(q, out, in_, reads=(), writes=()):
            return P.add(q, 'dma_start', dict(out=out, in_=in_), reads, writes, dma=True)

        cst = sb(top, "cst", [128, 640])
        DMA('sp', cst[:], cst_d[:, :], writes=['cst'])
        ident = cst[:, 0:128]
        triU = cst[:, 128:256]
        NEGI = cst[:, 256:384]
        NEGS = cst[:, 384:512]
        NEGTI = cst[:, 512:640]
        ones = sb(top, "ones", [128, 128])
        A('pool', 'memset', [], ['ones'], ap=ones[:], constant=1.0)

        def bcast_load(st, name, src, n):
            t = sb(st, name, [128, n])
            DMA('sp', t[:], src.partition_broadcast(128), writes=[name])
            return t

        def layer_norm(st_tiles, xin, xin_key, T, g_t, g_key, b_t, b_key, out, out_key):
            stats, mv, sd = st_tiles
            for c in range(4):
                A('dve', 'bn_stats', [xin_key], ['ln_stats'], out=stats[:T, c, :], in_=xin[:T, c * 512:(c + 1) * 512])
            A('dve', 'bn_aggr', ['ln_stats'], ['ln_mv'], out=mv[:T, :], in_=stats[:T, :, :])
            A('act', 'activation', ['ln_mv', 'eps_ln'], ['ln_sd'], out=sd[:T, 0:1], in_=mv[:T, 1:2], func=AF.Sqrt, bias=eps_ln[:T, :], scale=1.0)
            A('dve', 'reciprocal', ['ln_sd'], ['ln_rs'], out=sd[:T, 1:2], in_=sd[:T, 0:1])
            A('dve', 'tensor_scalar', [xin_key, 'ln_mv', 'ln_rs'], [out_key], out=out[:T, :], in0=xin[:T, :],
              scalar1=mv[:T, 0:1], scalar2=sd[:T, 1:2], op0=ALU.subtract, op1=ALU.mult)
            A('pool', 'tensor_tensor', [out_key, g_key], [out_key], out=out[:T, :], in0=out[:T, :], in1=g_t[:T, :], op=ALU.mult)
            A('pool', 'tensor_tensor', [out_key, b_key], [out_key], out=out[:T, :], in0=out[:T, :], in1=b_t[:T, :], op=ALU.add)

        eps_ln = sb(top, "eps_ln", [128, 1])
        A('pool', 'memset', [], ['eps_ln'], ap=eps_ln[:], constant=LN_EPS)

        def transpose_to(src, src_key, T, dstT, dst_key, col0, nkc=16):
            for g in range(0, nkc, 4):
                pt, pk = ps()
                n = min(4, nkc - g)
                for j in range(n):
                    kc = g + j
                    A('pe', 'transpose', [src_key, 'cst'], [pk], out=pt[:, j * 128:j * 128 + T],
                      in_=src[:T, kc * 128:(kc + 1) * 128], identity=ident[:T, :T])
                A('act', 'copy', [pk], [dst_key], out=dstT[:, g:g + n, col0:col0 + T],
                  in_=pt[:, 0:n * 128].rearrange("p (j t) -> p j t", t=128)[:, :, 0:T])

        def stage1():
            with ExitStack() as st:
                g0 = bcast_load(st, "ln0g", ln0g_d, D)
                b0 = bcast_load(st, "ln0b", ln0b_d, D)
                xts = [sb(st, f"s1x{i}", [128, D]) for i in range(2)]
                hts = [sb(st, f"s1h{i}", [128, D]) for i in range(2)]
                hT = sb16(st, "s1hT", [128, 16, 512])
                wfm = [sb(st, f"s1wf{i}", [128, 16, 128]) for i in range(2)]
                wtm = [sb(st, f"s1wt{i}", [128, 16, 512]) for i in range(2)]
                wfm16 = [sb16(st, f"s1wfb{i}", [128, 16, 128]) for i in range(2)]
                wtm16 = [sb16(st, f"s1wtb{i}", [128, 16, 512]) for i in range(2)]
                ofm = [sb(st, f"s1of{i}", [128, 512]) for i in range(2)]
                otm = [sb(st, f"s1ot{i}", [128, 512]) for i in range(3)]
                lnt = (sb(st, "ln_stats", [128, 4, 6]), sb(st, "ln_mv", [128, 2]), sb(st, "ln_sd", [128, 2]))
                xr = Rot([0, 1]); hr = Rot([0, 1]); wfr = Rot([0, 1]); wtr = Rot([0, 1]); ofr = Rot([0, 1]); otr = Rot([0, 1, 2])
                groups = []
                for s in seqs:
                    if s['sample'] is None:
                        for t0 in range(0, s['L'], 512):
                            n = min(512, s['L'] - t0)
                            groups.append((s['off'] + t0, n, [(s, t0, 0, n)]))
                if NS:
                    segs = [(s, 0, i * LS, LS) for i, s in enumerate(seqs[NP:])]
                    groups.append((seqs[NP]['off'], NS * LS, segs))
                fm_cols = [c * 128 for c in range(48)] + [8224 + c * 128 for c in range(8)] + [9248 + c * 128 for c in range(8)]
                tm_blocks = []
                for c in range(4):
                    tm_blocks.append((6144 + c * 512, 512, c * 512, AF.Silu))
                for c in range(4):
                    tm_blocks.append((10272 + c * 512, 512, 2048 + c * 512, AF.Identity))
                for c in range(4):
                    tm_blocks.append((12320 + c * 512, 512, 4096 + c * 512, AF.Sigmoid))
                for c in range(8):
                    tm_blocks.append((14384 + c * 512, 512, 6144 + c * 512, AF.Sigmoid))
                tm_blocks.append((8192, 16, 10240, AF.Sigmoid))
                tm_blocks.append((8208, 16, 10256, AF.Identity))
                tm_blocks.append((14368, 16, 10272, AF.Identity))
                win_v = win_d.rearrange("(kc kp) c -> kp kc c", kp=128)
                for (tok0, ntok, segs) in groups:
                    tiles = [(a, min(128, ntok - a)) for a in range(0, ntok, 128)]
                    for (a, T) in tiles:
                        xi = xr.next(); hi = hr.next()
                        xt = xts[xi]; ht = hts[hi]
                        DMA('sp', xt[:T, :], x_d[tok0 + a:tok0 + a + T, :], writes=[('s1x', xi)])
                        layer_norm(lnt, xt, ('s1x', xi), T, g0, "ln0g", b0, "ln0b", ht, ('s1h', hi))
                        DMA('sp', H_s[tok0 + a:tok0 + a + T, :], ht[:T, :], reads=[('s1h', hi)])
                        transpose_to(ht, ('s1h', hi), T, hT, 's1hT', a)
                    for bi, c0 in enumerate(fm_cols):
                        wi = wfr.next(); w = wfm[wi]
                        DMA('act', w[:], win_v[:, :, c0:c0 + 128], writes=[('s1wf', wi)])
                        A('pool', 'tensor_copy', [('s1wf', wi)], [('s1wfb', wi)], out=wfm16[wi][:], in_=w[:])
                        w = wfm16[wi]
                        pt, pk = ps()
                        for kc in range(16):
                            A('pe', 'matmul', [('s1wfb', wi), 's1hT'], [pk], out=pt[:, :ntok], lhsT=w[:, kc, :], rhs=hT[:, kc, :ntok],
                              start=(kc == 0), stop=(kc == 15))
                        oi = ofr.next(); o = ofm[oi]
                        A('dve', 'tensor_copy', [pk], [('s1of', oi)], out=o[:, :ntok], in_=pt[:, :ntok])
                        for (s, t0, gc, n) in segs:
                            col = s['cb'] + 3 + t0
                            DMA('sp', QKVT_s[bi * 128:(bi + 1) * 128, col:col + n], o[:, gc:gc + n], reads=[('s1of', oi)])
                    for (c0, ncol, pc0, func) in tm_blocks:
                        wi = wtr.next(); w = wtm[wi]
                        DMA('act', w[:, :, :ncol], win_v[:, :, c0:c0 + ncol], writes=[('s1wt', wi)])
                        A('dve', 'tensor_copy', [('s1wt', wi)], [('s1wtb', wi)], out=wtm16[wi][:, :, :ncol], in_=w[:, :, :ncol])
                        w = wtm16[wi]
                        for (a, T) in tiles:
                            pt, pk = ps()
                            for kc in range(16):
                                A('pe', 'matmul', [('s1wtb', wi), 's1hT'], [pk], out=pt[:T, :ncol], lhsT=hT[:, kc, a:a + T],
                                  rhs=w[:, kc, :ncol], start=(kc == 0), stop=(kc == 15))
                            oi = otr.next(); o = otm[oi]
                            A('act', 'activation', [pk], [('s1ot', oi)], out=o[:T, :ncol], in_=pt[:T, :ncol], func=func)
                            DMA('sp', PTM_s[tok0 + a:tok0 + a + T, pc0:pc0 + ncol], o[:T, :ncol], reads=[('s1ot', oi)])
                P.barrier()

        def DMAs(q, out, in_, reads=(), writes=()):
            return P.add(q, 'dma_start', dict(out=out, in_=in_, allow_slow_non_contiguous=True), reads, writes, dma=True)

        def stage2():
            with ExitStack() as st:
                cw = sb(st, "cw", [128, 48, 4])
                craw = sb(st, "craw", [4, 6144])
                cpad = sb(st, "cpad", [128, 48, 3])
                ocv = craw
                snr = sb(st, "snr", [8, 128])
                DMA('sp', craw[:, :], convw_d[:, :], writes=['craw'])
                pt, pk = ps()
                for blk in range(48):
                    A('pe', 'matmul', ['craw', 'cst'], [pk], out=pt[:, blk * 4:blk * 4 + 4], lhsT=craw[0:4, blk * 128:(blk + 1) * 128], rhs=ident[0:4, 0:4], start=True, stop=True)
                A('act', 'copy', [pk], ['cw'], out=cw[:, :, :], in_=pt[:, 0:192].rearrange("p (b w) -> p b w", w=4))
                dtb = bcast_load(st, "dtb", dtb_d, 16)
                aexp = bcast_load(st, "aexp", alog_d, 16)
                A('act', 'activation', ['aexp'], ['aexp'], out=aexp[:], in_=aexp[:], func=AF.Exp)
                gnw = bcast_load(st, "gnw", gnw_d, 128)
                mnw = bcast_load(st, "mnw", mnw_d, 256)
                bi = bcast_load(st, "bi", bi_d, 8)
                bf = bcast_load(st, "bf", bf_d, 8)
                eps6 = sb(st, "eps6", [128, 1]); A('pool', 'memset', [], ['eps6'], ap=eps6[:], constant=1e-6)
                one1 = sb(st, "one1", [128, 1]); A('pool', 'memset', [], ['one1'], ap=one1[:], constant=1.0)
                zt = sb(st, "zt", [128, D]); vmt = sb(st, "vmt", [128, D]); omt = sb(st, "omt", [128, D])
                sm48 = sb(st, "sm48", [128, 48])
                win = sb(st, "win", [128, 16, 131])
                qmk = sb(st, "qmk", [128, 16, 128])
                cv = [sb(st, f"cv{i}", [128, 16, 128]) for i in range(3)]
                ctmp = sb(st, "ctmp", [128, 16, 128])
                ktm = sb(st, "ktm", [128, 16, 128]); vtm = sb(st, "vtm", [128, 16, 128])
                S = sb(st, "S", [128, 16, 128]); Cx = sb(st, "Cx", [128, 8, 257]); mrep = sb(st, "mrep", [128, 8])
                Gbc = sb(st, "Gbc", [128, 16, 128]); eGbc = sb(st, "eGbc", [128, 16, 128]); grep = sb(st, "grep", [128, 16, 128])
                sm = sb(st, "smallt", [128, 256])
                g1 = sm[:, 0:16]; Gtm = sm[:, 16:32]; negG = sm[:, 32:48]; bexpG = sm[:, 48:64]; kdsc = sm[:, 64:80]
                nbeta = sm[:, 80:96]; ssq = sm[:, 96:112]; ig = sm[:, 112:120]; lf = sm[:, 120:128]; nlf = sm[:, 128:136]
                Ftm = sm[:, 136:144]; rr = sm[:, 144:152]; ssqb = sm[:, 152:160]; ksc = sm[:, 160:168]
                V3 = sb(st, "V3", [128, 24])
                hw = sb(st, "hwork", [128, 16, 128])
                Es = hw[:, 0, :]; ETi = hw[:, 1, :]; w1 = hw[:, 2, :]; w2 = hw[:, 3, :]; attnT = hw[:, 4, :]
                Pb = [hw[:, 5, :], hw[:, 6, :]]; PTb = [hw[:, 7, :], hw[:, 8, :]]; TTb = [hw[:, 9, :], hw[:, 10, :]]
                rv = hw[:, 11, :]; rk = hw[:, 12, :]; wT = hw[:, 13, :]; upre = hw[:, 14, :]; usb = hw[:, 15, :]
                hw2 = sb(st, "hwork2", [128, 8, 128])
                qg = hw2[:, 0, :]; kdec = hw2[:, 1, :]; dl = hw2[:, 2, :]; ww = hw2[:, 3, :]; wi = hw2[:, 4, :]; wiT = hw2[:, 5, :]
                kend = hw2[:, 6, :]; sel = hw2[:, 7, :]
                nq = sb(st, "nq", [128, 257])
                pp = sb(st, "pp", [128, 16])
                osb = sb(st, "osb", [128, D]); hsb = sb(st, "hsb", [128, D])
                oab = [sb(st, "oa", [128, D]), sb(st, "ob", [128, D])]
                wz = zt
                oT = ctmp
                lastbc = sb(st, "lastbc", [128, 48])
                igrep = grep[:, 0:8, :]; nlfrep = grep[:, 8:16, :]

                def hk(name, h=None):
                    return name if h is None else (name, h)

                for s in seqs:
                    T = s['T']; L = s['L']; q = s['q']; smp = s['sample']
                    nsq = {128: 6, 64: 5, 32: 4, 16: 3}[T]
                    if smp is None:
                        A('pool', 'memset', [], ['S'], ap=S[:], constant=0.0)
                        A('pool', 'memset', [], ['Cx'], ap=Cx[:], constant=0.0)
                        A('pool', 'memset', [], ['mrep'], ap=mrep[:], constant=0.0)
                    else:
                        DMA('sp', S[:], ss_d[smp].rearrange("h k v -> k h v"), writes=['S'])
                        DMA('sp', Cx[:, :, 0:256], sc_d[smp].rearrange("h k v -> k h v"), writes=['Cx'])
                        DMA('sp', snr[:, :], sn_d[smp], writes=['snr'])
                        pt, pk = ps()
                        A('pe', 'matmul', ['snr', 'cst'], [pk], out=pt[:, 0:8], lhsT=snr[0:8, :], rhs=ident[0:8, 0:8], start=True, stop=True)
                        A('act', 'copy', [pk], ['Cx'], out=Cx[:, :, 256], in_=pt[:, 0:8])
                        DMA('sp', craw[0:3, :], sconv_d[smp], reads=[], writes=['craw'])
                        pt, pk = ps()
                        for blk in range(48):
                            A('pe', 'matmul', ['craw', 'cst'], [pk], out=pt[:, blk * 3:blk * 3 + 3], lhsT=craw[0:3, blk * 128:(blk + 1) * 128], rhs=ident[0:3, 0:3], start=True, stop=True)
                        A('act', 'copy', [pk], ['cpad'], out=cpad[:, :, :], in_=pt[:, 0:144].rearrange("p (b w) -> p b w", w=3))
                        DMA('sp', mrep[:], sm_d[smp].partition_broadcast(128), writes=['mrep'])
                    A('pool', 'tensor_copy', ['cst'], ['sel'], out=sel[:T, :], in_=ident[:T, T - 1:T].to_broadcast([T, 128]))
                    for k in range(L // T):
                        tok0 = s['off'] + k * T
                        col0 = s['cb'] + 3 + k * T
                        DMA('sp', zt[:T, :], PTM_s[tok0:tok0 + T, 0:2048], writes=['zt'])
                        DMA('sp', vmt[:T, :], PTM_s[tok0:tok0 + T, 2048:4096], writes=['vmt'])
                        DMA('sp', omt[:T, :], PTM_s[tok0:tok0 + T, 4096:6144], writes=['omt'])
                        DMA('sp', sm48[:T, :], PTM_s[tok0:tok0 + T, 10240:10288], writes=['sm48'])
                        beta = sm48[:, 0:16]
                        DMA('act', qmk[:, :, :T], QKVT_s[6144:8192, col0:col0 + T].rearrange("(b p) c -> p b c", p=128), writes=['qmk'])
                        for gi in range(3):
                            b0 = gi * 16
                            src = QKVT_s[b0 * 128:(b0 + 16) * 128, :].rearrange("(b p) c -> p b c", p=128)
                            if k == 0:
                                if smp is None:
                                    A('pool', 'memset', [], ['win'], ap=win[:, :, 0:3], constant=0.0)
                                else:
                                    A('pool', 'tensor_copy', ['cpad'], ['win'], out=win[:, :, 0:3], in_=cpad[:, b0:b0 + 16, :])
                                DMA('act', win[:, :, 3:3 + T], src[:, :, col0:col0 + T], writes=['win'])
                            else:
                                DMA('act', win[:, :, 0:3 + T], src[:, :, col0 - 3:col0 + T], writes=['win'])
                            if k == L // T - 1:
                                for g4 in range(0, 16, 4):
                                    pt, pk = ps()
                                    for j in range(4):
                                        A('pe', 'matmul', ['win', 'cst'], [pk], out=pt[0:3, j * 128:(j + 1) * 128], lhsT=win[:, g4 + j, T:T + 3], rhs=ident[:, :], start=True, stop=True)
                                    A('act', 'copy', [pk], ['craw'], out=ocv[0:3, (b0 + g4) * 128:(b0 + g4 + 4) * 128], in_=pt[0:3, :])
                            dst = cv[gi]; dk = f"cv{gi}"
                            A('dve', 'tensor_tensor', ['win', 'cw'], [dk], out=dst[:, :, :T], in0=win[:, :, 0:T],
                              in1=cw[:, b0:b0 + 16, 0:1].to_broadcast([128, 16, T]), op=ALU.mult)
                            for w in range(1, 4):
                                A('pool', 'tensor_tensor', ['win', 'cw'], ['ctmp'], out=ctmp[:, :, :T], in0=win[:, :, w:w + T],
                                  in1=cw[:, b0:b0 + 16, w:w + 1].to_broadcast([128, 16, T]), op=ALU.mult)
                                A('dve', 'tensor_tensor', [dk, 'ctmp'], [dk], out=dst[:, :, :T], in0=dst[:, :, :T], in1=ctmp[:, :, :T], op=ALU.add)
                            A('act', 'activation', [dk], [dk], out=dst[:, :, :T], in_=dst[:, :, :T], func=AF.Silu)
                        for gi in range(2):
                            dst = cv[gi]; dk = f"cv{gi}"
                            A('dve', 'tensor_tensor', [dk], ['ctmp'], out=ctmp[:, :, :T], in0=dst[:, :, :T], in1=dst[:, :, :T], op=ALU.mult)
                            for g4 in range(0, 16, 4):
                                pt, pk = ps()
                                for j in range(4):
                                    A('pe', 'matmul', ['ones', 'ctmp'], [pk], out=pt[:, j * T:(j + 1) * T], lhsT=ones[:, :], rhs=ctmp[:, g4 + j, :T],
                                      start=True, stop=True)
                                A('act', 'activation', [pk, 'eps6'], ['ctmp'], out=ctmp[:, g4:g4 + 4, :T],
                                  in_=pt[:, 0:4 * T].rearrange("p (j t) -> p j t", t=T), func=AF.Sqrt, bias=eps6[:, :], scale=1.0)
                            A('dve', 'reciprocal', ['ctmp'], ['ctmp'], out=ctmp[:, :, :T], in_=ctmp[:, :, :T])
                            A('dve', 'scalar_tensor_tensor', [dk, 'ctmp'], [dk], out=dst[:, :, :T], in0=dst[:, :, :T],
                              scalar=(128.0 ** -0.5 if gi == 0 else 1.0), in1=ctmp[:, :, :T], op0=ALU.mult, op1=ALU.mult)
                        for (srcT, sk, dstm, dkey) in ((cv[1], 'cv1', ktm, 'ktm'), (cv[2], 'cv2', vtm, 'vtm')):
                            for g4 in range(0, 16, 4):
                                pt, pk = ps()
                                for j in range(4):
                                    A('pe', 'transpose', [sk, 'cst'], [pk], out=pt[:T, j * 128:(j + 1) * 128], in_=srcT[:, g4 + j, :T], identity=ident[:, :])
                                A('act', 'copy', [pk], [dkey], out=dstm[:T, g4:g4 + 4, :], in_=pt[:T, :].rearrange("t (j d) -> t j d", d=128))
                        A('dve', 'tensor_tensor', ['sm48', 'dtb'], ['g1'], out=g1[:T, :], in0=sm48[:T, 16:32], in1=dtb[:T, :], op=ALU.add)
                        A('act', 'activation', ['g1'], ['g1'], out=g1[:T, :], in_=g1[:T, :], func=AF.Exp)
                        A('act', 'activation', ['g1', 'one1'], ['g1'], out=g1[:T, :], in_=g1[:T, :], func=AF.Ln, bias=one1[:T, :], scale=1.0)
                        A('dve', 'scalar_tensor_tensor', ['g1', 'aexp'], ['g1'], out=g1[:T, :], in0=g1[:T, :], scalar=-1.0, in1=aexp[:T, :],
                          op0=ALU.mult, op1=ALU.mult)
                        A('pool', 'tensor_copy', ['g1'], ['grep'], out=grep[:T, :, :], in_=g1[:T, :].unsqueeze(2).to_broadcast([T, 16, 128]))
                        pt, pk = ps()
                        A('pe', 'matmul', ['cst', 'g1'], [pk], out=pt[:T, 0:16], lhsT=triU[:T, :T], rhs=g1[:T, :], start=True, stop=True)
                        A('dve', 'tensor_copy', [pk], ['Gtm'], out=Gtm[:T, :], in_=pt[:T, 0:16])
                        for g4 in range(0, 16, 4):
                            pt, pk = ps()
                            for j in range(4):
                                A('pe', 'matmul', ['grep', 'cst'], [pk], out=pt[:, j * T:(j + 1) * T], lhsT=grep[:T, g4 + j, :], rhs=triU[:T, :T],
                                  start=True, stop=True)
                            A('dve', 'tensor_copy', [pk], ['Gbc'], out=Gbc[:, g4:g4 + 4, :T], in_=pt[:, 0:4 * T].rearrange("p (j t) -> p j t", t=T))
                            A('act', 'activation', [pk], ['eGbc'], out=eGbc[:, g4:g4 + 4, :T], in_=pt[:, 0:4 * T].rearrange("p (j t) -> p j t", t=T),
                              func=AF.Exp)
                        A('dve', 'tensor_scalar', ['Gtm'], ['negG'], out=negG[:T, :], in0=Gtm[:T, :], scalar1=-1.0, scalar2=None, op0=ALU.mult)
                        A('dve', 'tensor_scalar', ['sm48'], ['nbeta'], out=nbeta[:T, :], in0=beta[:T, :], scalar1=-1.0, scalar2=None, op0=ALU.mult)
                        A('act', 'activation', ['Gtm'], ['bexpG'], out=bexpG[:T, :], in_=Gtm[:T, :], func=AF.Exp)
                        A('dve', 'tensor_tensor', ['bexpG', 'sm48'], ['bexpG'], out=bexpG[:T, :], in0=bexpG[:T, :], in1=beta[:T, :], op=ALU.mult)
                        A('dve', 'tensor_tensor', ['Gbc', 'Gtm'], ['kdsc'], out=kdsc[:T, :], in0=Gbc[:T, :, T - 1], in1=Gtm[:T, :], op=ALU.subtract)
                        A('act', 'activation', ['kdsc'], ['kdsc'], out=kdsc[:T, :], in_=kdsc[:T, :], func=AF.Exp)
                        A('pool', 'tensor_tensor', ['zt', 'gnw'], ['zt'], out=wz[:T, :].rearrange("t (h v) -> t h v", v=128),
                          in0=zt[:T, :].rearrange("t (h v) -> t h v", v=128), in1=gnw[:T, :].unsqueeze(1).to_broadcast([T, 16, 128]), op=ALU.mult)
                        for h in range(16):
                            kT = cv[1][:, h, :T]; qT = cv[0][:, h, :T]
                            p1, k1 = ps()
                            A('pe', 'matmul', ['cv1'], [k1], out=p1[:T, :T], lhsT=kT, rhs=kT, start=True, stop=True)
                            p2, k2 = ps()
                            A('pe', 'matmul', ['cv1', 'cv0'], [k2], out=p2[:T, :T], lhsT=kT, rhs=qT, start=True, stop=True)
                            A('dve', 'scalar_tensor_tensor', ['Gbc', 'cst'], ['w1'], out=w1[:T, :T], in0=Gbc[:T, h, :T], scalar=-1.0, in1=NEGS[:T, :T],
                              op0=ALU.mult, op1=ALU.add)
                            A('act', 'activation', ['w1', 'Gtm'], ['Es'], out=Es[:T, :T], in_=w1[:T, :T], func=AF.Exp, bias=Gtm[:T, h:h + 1], scale=1.0)
                            A('dve', 'tensor_tensor', ['Gbc', 'cst'], ['w2'], out=w2[:T, :T], in0=Gbc[:T, h, :T], in1=NEGTI[:T, :T], op=ALU.add)
                            A('act', 'activation', ['w2', 'negG'], ['ETi'], out=ETi[:T, :T], in_=w2[:T, :T], func=AF.Exp, bias=negG[:T, h:h + 1], scale=1.0)
                            A('dve', 'scalar_tensor_tensor', [k1, 'nbeta', 'Es'], ['P0'], out=Pb[0][:T, :T], in0=p1[:T, :T], scalar=nbeta[:T, h:h + 1],
                              in1=Es[:T, :T], op0=ALU.mult, op1=ALU.mult)
                            A('dve', 'tensor_tensor', [k2, 'ETi'], ['attnT'], out=attnT[:T, :T], in0=p2[:T, :T], in1=ETi[:T, :T], op=ALU.mult)
                            p3, k3 = ps()
                            A('pe', 'transpose', ['P0', 'cst'], [k3], out=p3[:T, :T], in_=Pb[0][:T, :T], identity=ident[:T, :T])
                            A('act', 'copy', [k3], ['PT0'], out=PTb[0][:T, :T], in_=p3[:T, :T])
                            A('dve', 'tensor_tensor', [k3, 'cst'], ['TT0'], out=TTb[0][:T, :T], in0=p3[:T, :T], in1=ident[:T, :T], op=ALU.add)
                            cur = 0
                            for m in range(1, nsq + 1):
                                nx = 1 - cur
                                pa, ka = ps()
                                A('pe', 'matmul', [f'P{cur}', f'PT{cur}'], [ka], out=pa[:T, :T], lhsT=PTb[cur][:T, :T], rhs=Pb[cur][:T, :T], start=True, stop=True)
                                if m < nsq:
                                    pb_, kb_ = ps()
                                    A('pe', 'matmul', [f'P{cur}', f'PT{cur}'], [kb_], out=pb_[:T, :T], lhsT=Pb[cur][:T, :T], rhs=PTb[cur][:T, :T], start=True, stop=True)
                                A('act', 'copy', [ka], [f'P{nx}'], out=Pb[nx][:T, :T], in_=pa[:T, :T])
                                if m < nsq:
                                    A('dve', 'tensor_copy', [kb_], [f'PT{nx}'], out=PTb[nx][:T, :T], in_=pb_[:T, :T])
                                pc, kc_ = ps()
                                A('pe', 'matmul', [f'P{nx}', f'TT{cur}'], [kc_], out=pc[:T, :T], lhsT=Pb[nx][:T, :T], rhs=TTb[cur][:T, :T], start=True, stop=True)
                                A('dve', 'tensor_tensor', [kc_, f'TT{cur}'], [f'TT{nx}'], out=TTb[nx][:T, :T], in0=pc[:T, :T], in1=TTb[cur][:T, :T], op=ALU.add)
                                cur = nx
                            TT = TTb[cur]; tk = f'TT{cur}'
                            A('pool', 'tensor_scalar', ['vtm', 'sm48'], ['rv'], out=rv[:T, :], in0=vtm[:T, h, :], scalar1=beta[:T, h:h + 1], scalar2=None, op0=ALU.mult)
                            A('pool', 'tensor_scalar', ['ktm', 'bexpG'], ['rk'], out=rk[:T, :], in0=ktm[:T, h, :], scalar1=bexpG[:T, h:h + 1], scalar2=None, op0=ALU.mult)
                            pu, ku = ps()
                            A('pe', 'matmul', [tk, 'rv'], [ku], out=pu[:T, 0:128], lhsT=TT[:T, :T], rhs=rv[:T, :], start=True, stop=True)
                            pw, kw_ = ps()
                            A('pe', 'matmul', [tk, 'rk'], [kw_], out=pw[:, :T], lhsT=rk[:T, :], rhs=TT[:T, :T], start=True, stop=True)
                            A('act', 'copy', [ku], ['upre'], out=upre[:T, :], in_=pu[:T, 0:128])
                            A('act', 'copy', [kw_], ['wT'], out=wT[:, :T], in_=pw[:, :T])
                            pws, kws = ps()
                            A('pe', 'matmul', ['wT', 'S'], [kws], out=pws[:T, 0:128], lhsT=wT[:, :T], rhs=S[:, h, :], start=True, stop=True)
                            A('dve', 'tensor_tensor', ['upre', kws], ['usb'], out=usb[:T, :], in0=upre[:T, :], in1=pws[:T, 0:128], op=ALU.subtract)
                            A('pool', 'tensor_tensor', ['cv0', 'eGbc'], ['qg'], out=qg[:, :T], in0=qT, in1=eGbc[:, h, :T], op=ALU.mult)
                            po, ko = ps()
                            A('pe', 'matmul', ['qg', 'S'], [ko], out=po[:T, 0:128], lhsT=qg[:, :T], rhs=S[:, h, :], start=True, stop=False)
                            A('pe', 'matmul', ['attnT', 'usb'], [ko], out=po[:T, 0:128], lhsT=attnT[:T, :T], rhs=usb[:T, :], start=False, stop=True)
                            A('act', 'copy', [ko], ['osb'], out=osb[:T, h * 128:(h + 1) * 128], in_=po[:T, 0:128])
                            A('pool', 'tensor_scalar', ['ktm', 'kdsc'], ['kdec'], out=kdec[:T, :], in0=ktm[:T, h, :], scalar1=kdsc[:T, h:h + 1], scalar2=None, op0=ALU.mult)
                            pS, kS = ps()
                            A('pe', 'matmul', ['kdec', 'usb'], [kS], out=pS[:, 0:128], lhsT=kdec[:T, :], rhs=usb[:T, :], start=True, stop=True)
                            A('dve', 'scalar_tensor_tensor', ['S', 'eGbc', kS], ['S'], out=S[:, h, :], in0=S[:, h, :], scalar=eGbc[:, h, T - 1:T], in1=pS[:, 0:128],
                              op0=ALU.mult, op1=ALU.add)
                        A('pool', 'tensor_tensor', ['osb'], ['hsb'], out=hsb[:T, :], in0=osb[:T, :], in1=osb[:T, :], op=ALU.mult)
                        A('dve', 'tensor_reduce', ['hsb'], ['ssq'], out=ssq[:T, :], in_=hsb[:T, :].rearrange("t (h v) -> t h v", v=128), axis=AX.X, op=ALU.add)
                        A('act', 'activation', ['ssq', 'eps6'], ['ssq'], out=ssq[:T, :], in_=ssq[:T, :], func=AF.Sqrt, bias=eps6[:T, :], scale=1.0 / 128.0)
                        A('dve', 'reciprocal', ['ssq'], ['ssq'], out=ssq[:T, :], in_=ssq[:T, :])
                        A('dve', 'tensor_tensor', ['osb', 'ssq'], ['osb'], out=osb[:T, :].rearrange("t (h v) -> t h v", v=128),
                          in0=osb[:T, :].rearrange("t (h v) -> t h v", v=128), in1=ssq[:T, :].unsqueeze(2).to_broadcast([T, 16, 128]), op=ALU.mult)
                        A('dve', 'tensor_tensor', ['osb', 'zt'], ['oa'], out=oab[0][:T, :], in0=osb[:T, :], in1=wz[:T, :], op=ALU.mult)
                        A('dve', 'tensor_tensor', ['sm48', 'bi'], ['ig'], out=ig[:T, :], in0=sm48[:T, 32:40], in1=bi[:T, :], op=ALU.add)
                        A('dve', 'tensor_tensor', ['sm48', 'bf'], ['nlf'], out=nlf[:T, :], in0=sm48[:T, 40:48], in1=bf[:T, :], op=ALU.add)
                        A('act', 'activation', ['nlf'], ['nlf'], out=nlf[:T, :], in_=nlf[:T, :], func=AF.Exp, scale=-1.0)
                        A('act', 'activation', ['nlf', 'one1'], ['nlf'], out=nlf[:T, :], in_=nlf[:T, :], func=AF.Ln, bias=one1[:T, :], scale=1.0)
                        A('dve', 'tensor_scalar', ['nlf'], ['lf'], out=lf[:T, :], in0=nlf[:T, :], scalar1=-1.0, scalar2=None, op0=ALU.mult)
                        pt, pk = ps()
                        A('pe', 'matmul', ['cst', 'lf'], [pk], out=pt[:T, 0:8], lhsT=triU[:T, :T], rhs=lf[:T, :], start=True, stop=True)
                        A('dve', 'tensor_copy', [pk], ['V3'], out=V3[:T, 16:24], in_=pt[:T, 0:8])
                        Fm = V3[:, 16:24]
                        A('dve', 'tensor_tensor', ['ig', 'V3'], ['rr'], out=rr[:T, :], in0=ig[:T, :], in1=Fm[:T, :], op=ALU.subtract)
                        A('pool', 'tensor_copy', ['ig'], ['grep'], out=igrep[:T, :, :], in_=ig[:T, :].unsqueeze(2).to_broadcast([T, 8, 128]))
                        A('pool', 'tensor_copy', ['nlf'], ['grep'], out=nlfrep[:T, :, :], in_=nlf[:T, :].unsqueeze(2).to_broadcast([T, 8, 128]))
                        qmT = qmk[:, 0:8, :]; kmT = qmk[:, 8:16, :]
                        for h in range(8):
                            pr, kr = ps()
                            A('pe', 'matmul', ['grep', 'cst'], [kr], out=pr[:, :T], lhsT=igrep[:T, h, :], rhs=ident[:T, :T], start=True, stop=False)
                            A('pe', 'matmul', ['grep', 'cst'], [kr], out=pr[:, :T], lhsT=nlfrep[:T, h, :], rhs=triU[:T, :T], start=False, stop=True)
                            A('dve', 'tensor_tensor', [kr, 'cst'], ['dl'], out=dl[:T, :T], in0=pr[:T, :T], in1=NEGI[:T, :T], op=ALU.add)
                            imp = V3[:, 8 + h:9 + h]
                            A('dve', 'tensor_reduce', ['dl'], [('imp', h)], out=imp[:T, :], in_=dl[:T, :T], axis=AX.X, op=ALU.max)
                            A('dve', 'tensor_scalar', [('imp', h)], ['pp0'], out=pp[:T, 0:1], in0=imp[:T, :], scalar1=-1.0, scalar2=None, op0=ALU.mult)
                            A('act', 'activation', ['dl', 'pp0'], ['ww'], out=ww[:T, :T], in_=dl[:T, :T], func=AF.Exp, bias=pp[:T, 0:1], scale=1.0)
                            pq, kq = ps()
                            A('pe', 'matmul', ['qmk'], [kq], out=pq[:T, :T], lhsT=qmT[:, h, :T], rhs=kmT[:, h, :T], start=True, stop=True)
                            A('dve', 'scalar_tensor_tensor', [kq, 'ww'], ['wi'], out=wi[:T, :T], in0=pq[:T, :T], scalar=128.0 ** -0.5, in1=ww[:T, :T],
                              op0=ALU.mult, op1=ALU.mult)
                            A('dve', 'tensor_reduce', ['wi'], ['pp1'], out=pp[:T, 1:2], in_=wi[:T, :T], axis=AX.X, op=ALU.add)
                            pT_, kT_ = ps()
                            A('pe', 'transpose', ['wi', 'cst'], [kT_], out=pT_[:T, :T], in_=wi[:T, :T], identity=ident[:T, :T])
                            A('act', 'copy', [kT_], ['wiT'], out=wiT[:T, :T], in_=pT_[:T, :T])
                            pn, kn = ps()
                            A('pe', 'matmul', ['wiT', 'vmt'], [kn], out=pn[:T, 0:256], lhsT=wiT[:T, :T], rhs=vmt[:T, h * 256:(h + 1) * 256], start=True, stop=True)
                            pc, kc_ = ps()
                            A('pe', 'matmul', ['qmk', 'Cx'], [kc_], out=pc[:T, 0:257], lhsT=qmT[:, h, :T], rhs=Cx[:, h, :], start=True, stop=True)
                            A('dve', 'tensor_tensor', ['mrep', ('imp', h)], ['pp2'], out=pp[:T, 2:3], in0=mrep[:T, h:h + 1], in1=imp[:T, :], op=ALU.max)
                            A('dve', 'tensor_tensor', ['pp2', 'V3'], [('mt', h)], out=V3[:T, h:h + 1], in0=pp[:T, 2:3], in1=Fm[:T, h:h + 1], op=ALU.add)
                            A('dve', 'tensor_tensor', ['mrep', 'pp2'], ['pp34'], out=pp[:T, 3:4], in0=mrep[:T, h:h + 1], in1=pp[:T, 2:3], op=ALU.subtract)
                            A('dve', 'tensor_tensor', [('imp', h), 'pp2'], ['pp34'], out=pp[:T, 4:5], in0=imp[:T, :], in1=pp[:T, 2:3], op=ALU.subtract)
                            A('act', 'activation', ['pp34'], ['pp34'], out=pp[:T, 3:5], in_=pp[:T, 3:5], func=AF.Exp)
                            A('act', 'activation', [kc_, 'pp34'], ['nq'], out=nq[:T, :], in_=pc[:T, 0:257], func=AF.Identity, scale=pp[:T, 3:4])
                            A('dve', 'scalar_tensor_tensor', [kn, 'pp34', 'nq'], ['hsb'], out=hsb[:T, h * 256:(h + 1) * 256], in0=pn[:T, 0:256], scalar=pp[:T, 4:5],
                              in1=nq[:T, 0:256], op0=ALU.mult, op1=ALU.add)
                            A('dve', 'scalar_tensor_tensor', ['pp1', 'pp34', 'nq'], ['pp5'], out=pp[:T, 5:6], in0=pp[:T, 1:2], scalar=pp[:T, 4:5],
                              in1=nq[:T, 256:257], op0=ALU.mult, op1=ALU.add)
                            A('act', 'activation', [('mt', h)], ['pp6'], out=pp[:T, 6:7], in_=V3[:T, h:h + 1], func=AF.Exp, scale=-1.0)
                            A('act', 'activation', ['pp5'], ['pp5'], out=pp[:T, 5:6], in_=pp[:T, 5:6], func=AF.Abs)
                            A('dve', 'tensor_tensor', ['pp5', 'pp6'], ['pp7'], out=pp[:T, 7:8], in0=pp[:T, 5:6], in1=pp[:T, 6:7], op=ALU.max)
                            A('dve', 'reciprocal', ['pp7'], ['pp7'], out=pp[:T, 7:8], in_=pp[:T, 7:8])
                            A('dve', 'tensor_scalar', ['hsb', 'pp7'], ['hsb'], out=hsb[:T, h * 256:(h + 1) * 256], in0=hsb[:T, h * 256:(h + 1) * 256],
                              scalar1=pp[:T, 7:8], scalar2=None, op0=ALU.mult)
                        A('pool', 'tensor_tensor', ['hsb'], ['osb'], out=osb[:T, :], in0=hsb[:T, :], in1=hsb[:T, :], op=ALU.mult)
                        A('dve', 'tensor_reduce', ['osb'], ['ssqb'], out=ssqb[:T, :], in_=osb[:T, :].rearrange("t (h v) -> t h v", v=256), axis=AX.X, op=ALU.add)
                        A('act', 'activation', ['ssqb', 'eps6'], ['ssqb'], out=ssqb[:T, :], in_=ssqb[:T, :], func=AF.Sqrt, bias=eps6[:T, :], scale=1.0 / 256.0)
                        A('dve', 'reciprocal', ['ssqb'], ['ssqb'], out=ssqb[:T, :], in_=ssqb[:T, :])
                        A('dve', 'tensor_tensor', ['hsb', 'ssqb'], ['hsb'], out=hsb[:T, :].rearrange("t (h v) -> t h v", v=256),
                          in0=hsb[:T, :].rearrange("t (h v) -> t h v", v=256), in1=ssqb[:T, :].unsqueeze(2).to_broadcast([T, 8, 256]), op=ALU.mult)
                        A('pool', 'tensor_tensor', ['hsb', 'mnw'], ['hsb'], out=hsb[:T, :].rearrange("t (h v) -> t h v", v=256),
                          in0=hsb[:T, :].rearrange("t (h v) -> t h v", v=256), in1=mnw[:T, :].unsqueeze(1).to_broadcast([T, 8, 256]), op=ALU.mult)
                        A('dve', 'tensor_tensor', ['hsb', 'omt'], ['ob'], out=oab[1][:T, :], in0=hsb[:T, :], in1=omt[:T, :], op=ALU.mult)
                        plb, klb = ps()
                        A('pe', 'matmul', ['sel', 'V3'] + [('mt', h) for h in range(8)] + [('imp', h) for h in range(8)], [klb],
                          out=plb[:, 0:24], lhsT=sel[:T, :], rhs=V3[:T, 0:24], start=True, stop=True)
                        A('dve', 'tensor_copy', [klb], ['lastbc'], out=lastbc[:, 0:24], in_=plb[:, 0:24])
                        mnew = lastbc[:, 0:8]; impl = lastbc[:, 8:16]; Fl = lastbc[:, 16:24]
                        abc = lastbc[:, 24:32]; bbc = lastbc[:, 32:40]
                        A('dve', 'tensor_tensor', ['lastbc'], ['bbc'], out=bbc, in0=Fl, in1=mnew, op=ALU.subtract)
                        A('dve', 'tensor_tensor', ['bbc', 'mrep'], ['abc'], out=abc, in0=bbc, in1=mrep[:, :], op=ALU.add)
                        A('dve', 'tensor_tensor', ['bbc', 'lastbc'], ['bbc'], out=bbc, in0=bbc, in1=impl, op=ALU.add)
                        A('act', 'activation', ['abc', 'bbc'], ['abc', 'bbc'], out=lastbc[:, 24:40], in_=lastbc[:, 24:40], func=AF.Exp)
                        A('dve', 'tensor_scalar', ['bbc'], ['bbc'], out=bbc, in0=bbc, scalar1=128.0 ** -0.5, scalar2=None, op0=ALU.mult)
                        A('dve', 'tensor_tensor', ['rr', 'lastbc'], ['ksc'], out=ksc[:T, :], in0=rr[:T, :], in1=impl[:T, :], op=ALU.subtract)
                        A('act', 'activation', ['ksc'], ['ksc'], out=ksc[:T, :], in_=ksc[:T, :], func=AF.Exp)
                        for h in range(8):
                            pk_, kk_ = ps()
                            A('pe', 'transpose', ['qmk', 'cst'], [kk_], out=pk_[:T, 0:128], in_=kmT[:, h, :T], identity=ident[:, :])
                            A('dve', 'tensor_scalar', [kk_, 'ksc'], ['kend'], out=kend[:T, :], in0=pk_[:T, 0:128], scalar1=ksc[:T, h:h + 1], scalar2=None, op0=ALU.mult)
                            pu, ku = ps()
                            A('pe', 'matmul', ['kend', 'vmt'], [ku], out=pu[:, 0:256], lhsT=kend[:T, :], rhs=vmt[:T, h * 256:(h + 1) * 256], start=True, stop=True)
                            A('pe', 'matmul', ['kend', 'ones'], [ku], out=pu[:, 256:257], lhsT=kend[:T, :], rhs=ones[:T, 0:1], start=True, stop=True)
                            A('pool', 'tensor_scalar', ['Cx', 'abc'], ['Cx'], out=Cx[:, h, :], in0=Cx[:, h, :], scalar1=abc[:, h:h + 1], scalar2=None, op0=ALU.mult)
                            A('dve', 'scalar_tensor_tensor', [ku, 'bbc', 'Cx'], ['Cx'], out=Cx[:, h, :], in0=pu[:, 0:257], scalar=bbc[:, h:h + 1], in1=Cx[:, h, :],
                              op0=ALU.mult, op1=ALU.add)
                        A('dve', 'tensor_copy', ['lastbc', 'abc'], ['mrep'], out=mrep[:, :], in_=mnew)
                        for bi_ in range(2):
                            transpose_to(oab[bi_], ('oa', 'ob')[bi_], T, oT, 'ctmp', 0)
                            DMA('sp', OABT_s[bi_, :, :, tok0:tok0 + T].rearrange("k p t -> p k t"), oT[:, :, :T], reads=['ctmp'])
                    DMA('sp', os_d[q].rearrange("h k v -> k h v"), S[:], reads=['S'])
                    DMA('sp', oc_d[q].rearrange("h k v -> k h v"), Cx[:, :, 0:256], reads=['Cx'])
                    A('dve', 'tensor_copy', ['Cx'], ['nq'], out=nq[:, 0:8], in_=Cx[:, :, 256])
                    pt, pk = ps()
                    A('pe', 'matmul', ['nq', 'cst'], [pk], out=pt[0:8, 0:128], lhsT=nq[:, 0:8], rhs=ident[:, :], start=True, stop=True)
                    A('act', 'copy', [pk], ['snr'], out=snr[0:8, :], in_=pt[0:8, 0:128])
                    DMA('sp', on_d[q], snr[0:8, :], reads=['snr'])
                    DMA('sp', om_d[q:q + 1, :], mrep[0:1, :], reads=['mrep'])
                    DMA('sp', oconv_d[q], ocv[0:3, :], reads=['craw'])
                P.barrier()

        def token_groups():
            groups = []
            for s in seqs:
                if s['sample'] is None:
                    for t0 in range(0, s['L'], 512):
                        groups.append((s['off'] + t0, min(512, s['L'] - t0)))
            if NS:
                groups.append((seqs[NP]['off'], NS * LS))
            return groups

        def stage3():
            with ExitStack() as st:
                g1t = bcast_load(st, "ln1g", ln1g_d, D)
                b1t = bcast_load(st, "ln1b", ln1b_d, D)
                lnt = (sb(st, "ln_stats", [128, 4, 6]), sb(st, "ln_mv", [128, 2]), sb(st, "ln_sd", [128, 2]))
                bufA = sb(st, "bufA", [128, 16, 512]); bufB = sb(st, "bufB", [128, 16, 512])
                wbs = [sb(st, f"s3w{i}", [128, 16, 256]) for i in range(2)]
                wr = Rot([0, 1])
                mixed = sb(st, "mixed", [128, 4, D])
                x1t = sb(st, "x1t", [128, D])
                gab = [sb(st, f"gab{i}", [128, 2, 256]) for i in range(2)]
                gr = Rot([0, 1])
                hsl = [sb(st, f"hsl{i}", [128, 256]) for i in range(2)]
                hr = Rot([0, 1])
                keysT = sb(st, "keysT", [128, 16, 128])
                sc = sb(st, "sc", [128, 16, 128]); sc2 = sb(st, "sc2", [128, 16, 128]); cand2 = sb(st, "cand2", [128, 8, 256])
                t16 = sb(st, "t16", [128, 16, 16]); c16 = sb(st, "c16", [128, 8, 16]); ce = sb(st, "ce", [128, 8, 16])
                sm3 = sb(st, "sm3", [128, 64])
                Zt = sm3[:, 0:8]; lnZ = sm3[:, 8:16]; b0t = sm3[:, 16:24]; the = sm3[:, 24:32]; c1t = sm3[:, 32:40]
                pgt = sb(st, "pgt", [128, 24])
                DMA('sp', sc[:, :, :], keys_d.rearrange("h n d -> n h d"), writes=['sc'])
                for g4 in range(0, 16, 4):
                    pt, pk = ps()
                    for j in range(4):
                        A('pe', 'transpose', ['sc', 'cst'], [pk], out=pt[:, j * 128:(j + 1) * 128], in_=sc[:, g4 + j, :], identity=ident[:, :])
                    A('act', 'copy', [pk], ['keysT'], out=keysT[:, g4:g4 + 4, :], in_=pt[:, :].rearrange("p (j n) -> p j n", n=128))
                wa_v = wa_d.rearrange("(kc kp) c -> kp kc c", kp=128)
                wb_v = wb_d.rearrange("(kc kp) c -> kp kc c", kp=128)
                wo_v = wo_d.rearrange("(kc kp) c -> kp kc c", kp=128)
                wq_v = wq_d.rearrange("(kc kp) c -> kp kc c", kp=128)
                for (tok0, ntok) in token_groups():
                    tiles = [(a, min(128, ntok - a)) for a in range(0, ntok, 128)]
                    DMA('sp', bufA[:, :, :ntok], OABT_s[0, :, :, tok0:tok0 + ntok].rearrange("k p t -> p k t"), writes=['bufA'])
                    DMA('sp', bufB[:, :, :ntok], OABT_s[1, :, :, tok0:tok0 + ntok].rearrange("k p t -> p k t"), writes=['bufB'])
                    for cb in range(8):
                        c0 = cb * 256
                        ia = wr.next(); ib = wr.next()
                        DMA('act', wbs[ia][:], wa_v[:, :, c0:c0 + 256], writes=[('s3w', ia)])
                        DMA('act', wbs[ib][:], wb_v[:, :, c0:c0 + 256], writes=[('s3w', ib)])
                        for ti, (a, T) in enumerate(tiles):
                            gi = gr.next(); gt = gab[gi]
                            DMA('sp', gt[:T, 0, :], PTM_s[tok0 + a:tok0 + a + T, 6144 + c0:6144 + c0 + 256], writes=[('gab', gi)])
                            DMA('sp', gt[:T, 1, :], PTM_s[tok0 + a:tok0 + a + T, 8192 + c0:8192 + c0 + 256], writes=[('gab', gi)])
                            pa, ka = ps()
                            for kc in range(16):
                                A('pe', 'matmul', ['bufA', ('s3w', ia)], [ka], out=pa[:T, 0:256], lhsT=bufA[:, kc, a:a + T], rhs=wbs[ia][:, kc, :],
                                  start=(kc == 0), stop=(kc == 15))
                            pb_, kb_ = ps()
                            for kc in range(16):
                                A('pe', 'matmul', ['bufB', ('s3w', ib)], [kb_], out=pb_[:T, 0:256], lhsT=bufB[:, kc, a:a + T], rhs=wbs[ib][:, kc, :],
                                  start=(kc == 0), stop=(kc == 15))
                            A('dve', 'tensor_tensor', [ka, ('gab', gi)], [('mixed', ti)], out=mixed[:T, ti, c0:c0 + 256], in0=pa[:T, 0:256], in1=gt[:T, 0, :], op=ALU.mult)
                            A('dve', 'tensor_tensor', [kb_, ('gab', gi)], [('gab', gi)], out=gt[:T, 1, :], in0=pb_[:T, 0:256], in1=gt[:T, 1, :], op=ALU.mult)
                            A('pool', 'tensor_tensor', [('mixed', ti), ('gab', gi)], [('mixed', ti)], out=mixed[:T, ti, c0:c0 + 256], in0=mixed[:T, ti, c0:c0 + 256],
                              in1=gt[:T, 1, :], op=ALU.add)
                    for ti, (a, T) in enumerate(tiles):
                        transpose_to(mixed[:, ti, :], ('mixed', ti), T, bufA, 'bufA', a)
                    for cb in range(8):
                        c0 = cb * 256
                        iw = wr.next()
                        DMA('act', wbs[iw][:], wo_v[:, :, c0:c0 + 256], writes=[('s3w', iw)])
                        for ti, (a, T) in enumerate(tiles):
                            hi = hr.next()
                            DMA('sp', hsl[hi][:T, :], H_s[tok0 + a:tok0 + a + T, c0:c0 + 256], writes=[('hsl', hi)])
                            pa, ka = ps()
                            for kc in range(16):
                                A('pe', 'matmul', ['bufA', ('s3w', iw)], [ka], out=pa[:T, 0:256], lhsT=bufA[:, kc, a:a + T], rhs=wbs[iw][:, kc, :],
                                  start=(kc == 0), stop=(kc == 15))
                            A('dve', 'scalar_tensor_tensor', [('hsl', hi), ka], [('mixed', ti)], out=mixed[:T, ti, c0:c0 + 256], in0=hsl[hi][:T, :], scalar=ALPHA,
                              in1=pa[:T, 0:256], op0=ALU.mult, op1=ALU.add)
                    for ti, (a, T) in enumerate(tiles):
                        layer_norm(lnt, mixed[:, ti, :], ('mixed', ti), T, g1t, "ln1g", b1t, "ln1b", x1t, 'x1t')
                        DMA('sp', X1_s[tok0 + a:tok0 + a + T, :], x1t[:T, :], reads=['x1t'])
                        transpose_to(x1t, 'x1t', T, bufB, 'bufB', a)
                    DMA('sp', X1T_s[:, :, tok0:tok0 + ntok].rearrange("k p t -> p k t"), bufB[:, :, :ntok], reads=['bufB'])
                    for blk in range(16):
                        iw = wr.next()
                        DMA('act', wbs[iw][:, :, 0:128], wq_v[:, :, blk * 128:(blk + 1) * 128], writes=[('s3w', iw)])
                        pa, ka = ps()
                        for kc in range(16):
                            A('pe', 'matmul', ['bufB', ('s3w', iw)], [ka], out=pa[:, :ntok], lhsT=wbs[iw][:, kc, 0:128], rhs=bufB[:, kc, :ntok],
                              start=(kc == 0), stop=(kc == 15))
                        A('act', 'copy', [ka], ['bufA'], out=bufA[:, blk, :ntok], in_=pa[:, :ntok])
                    for ti, (a, T) in enumerate(tiles):
                        for g4 in range(0, 16, 4):
                            pt, pk = ps()
                            for j in range(4):
                                A('pe', 'matmul', ['bufA', 'keysT'], [pk], out=pt[:T, j * 128:(j + 1) * 128], lhsT=bufA[:, g4 + j, a:a + T], rhs=keysT[:, g4 + j, :],
                                  start=True, stop=True)
                            A('act', 'copy', [pk], ['sc'], out=sc[:T, g4:g4 + 4, :], in_=pt[:T, :].rearrange("t (j n) -> t j n", n=128))
                        for hp in range(16):
                            A('dve', 'max', ['sc'], ['t16'], out=t16[:T, hp, 0:8], in_=sc[:T, hp, :])
                            A('dve', 'match_replace', ['sc', 't16'], ['sc2'], out=sc2[:T, hp, :], in_to_replace=t16[:T, hp, 0:8], in_values=sc[:T, hp, :], imm_value=-1e30)
                            A('dve', 'max', ['sc2'], ['t16'], out=t16[:T, hp, 8:16], in_=sc2[:T, hp, :])
                        t16v = t16[:, :, :].rearrange("t (h p) k -> t h p k", p=2)
                        cand = sc2[:, :, :].rearrange("t (h x) n -> t h (x n)", x=2)
                        A('dve', 'tensor_tensor', ['t16'], ['sc2'], out=cand[:T, :, :].rearrange("t h (i j) -> t h i j", j=16),
                          in0=t16v[:T, :, 0, :].unsqueeze(3).to_broadcast([T, 8, 16, 16]), in1=t16v[:T, :, 1, :].unsqueeze(2).to_broadcast([T, 8, 16, 16]), op=ALU.add)
                        for h in range(8):
                            A('dve', 'max', ['sc2'], ['c16'], out=c16[:T, h, 0:8], in_=cand[:T, h, :])
                            A('dve', 'match_replace', ['sc2', 'c16'], ['cand2'], out=cand2[:T, h, :], in_to_replace=c16[:T, h, 0:8], in_values=cand[:T, h, :], imm_value=-1e30)
                            A('dve', 'max', ['cand2'], ['c16'], out=c16[:T, h, 8:16], in_=cand2[:T, h, :])
                        A('dve', 'tensor_tensor', ['c16'], ['ce'], out=ce[:T, :, :], in0=c16[:T, :, :], in1=c16[:T, :, 0:1].to_broadcast([T, 8, 16]), op=ALU.subtract)
                        A('act', 'activation', ['ce'], ['ce'], out=ce[:T, :, :], in_=ce[:T, :, :], func=AF.Exp)
                        A('dve', 'tensor_reduce', ['ce'], ['Zt'], out=Zt[:T, :], in_=ce[:T, :, :], axis=AX.X, op=ALU.add)
                        A('act', 'activation', ['Zt'], ['lnZ'], out=lnZ[:T, :], in_=Zt[:T, :], func=AF.Ln)
                        A('dve', 'tensor_copy', ['c16'], ['pgt'], out=pgt[:T, 0:8], in_=c16[:T, :, 15])
                        A('dve', 'tensor_tensor', ['t16', 'lnZ'], ['pgt'], out=pgt[:T, 8:16], in0=t16v[:T, :, 0, 0], in1=lnZ[:T, :], op=ALU.add)
                        A('dve', 'tensor_copy', ['t16'], ['pgt'], out=pgt[:T, 16:24], in_=t16v[:T, :, 1, 0])
                        scv = sc[:, :, :].rearrange("t (h p) n -> t h p n", p=2)
                        DMA('sp', PG_s[tok0 + a:tok0 + a + T, 0, :].rearrange("t (h n) -> t h n", n=128), scv[:T, :, 0, :], reads=['sc'])
                        DMA('sp', PG_s[tok0 + a:tok0 + a + T, 1, :].rearrange("t (h n) -> t h n", n=128), scv[:T, :, 1, :], reads=['sc'])
                        DMA('sp', PGT_s[tok0 + a:tok0 + a + T, :], pgt[:T, :], reads=['pgt'])
                P.barrier()

        def stage0():
            with ExitStack() as st:
                urs = [sb(st, f"s0u{i}", [128, D]) for i in range(3)]; ur = Rot([0, 1, 2])
                vrs = [sb(st, f"s0v{i}", [128, D]) for i in range(3)]; vr = Rot([0, 1, 2])
                uts = [sb16(st, f"s0ut{i}", [128, 16, 128]) for i in range(2)]; utr = Rot([0, 1])
                v16s = [sb16(st, f"s0v16{i}", [128, D]) for i in range(2)]; v16r = Rot([0, 1])
                for ab in range(128):
                    iu = ur.next(); iv = vr.next(); it = utr.next(); i16 = v16r.next()
                    DMA('sp', urs[iu][:, :], pu_d[ab * 128:(ab + 1) * 128, :], writes=[('s0u', iu)])
                    DMA('act', vrs[iv][:, :], pv_d[ab * 128:(ab + 1) * 128, :], writes=[('s0v', iv)])
                    for g4 in range(0, 16, 4):
                        pt, pk = ps()
                        for jj in range(4):
                            kc = g4 + jj
                            A('pe', 'transpose', [('s0u', iu), 'cst'], [pk], out=pt[:, jj * 128:(jj + 1) * 128], in_=urs[iu][:, kc * 128:(kc + 1) * 128], identity=ident[:, :])
                        if g4 % 8:
                            A('dve', 'tensor_copy', [pk], [('s0ut', it)], out=uts[it][:, g4:g4 + 4, :], in_=pt[:, :].rearrange("p (j e) -> p j e", e=128))
                        else:
                            A('act', 'copy', [pk], [('s0ut', it)], out=uts[it][:, g4:g4 + 4, :], in_=pt[:, :].rearrange("p (j e) -> p j e", e=128))
                    DMA('sp', UT16_s[ab, :, :], uts[it][:, :, :].rearrange("p k e -> p (k e)"), reads=[('s0ut', it)])
                    if ab % 2:
                        A('dve', 'tensor_copy', [('s0v', iv)], [('s0v16', i16)], out=v16s[i16][:, :], in_=vrs[iv][:, :])
                    else:
                        A('pool', 'tensor_copy', [('s0v', iv)], [('s0v16', i16)], out=v16s[i16][:, :], in_=vrs[iv][:, :])
                    DMA('sp', V16_s[ab * 128:(ab + 1) * 128, :], v16s[i16][:, :], reads=[('s0v16', i16)])
                P.barrier()

        def stage45():
            with ExitStack() as st:
                bufX16 = sb16(st, "bufX16", [128, 16, 512])
                acc = sb(st, "acc", [128, 4, D])
                wg_v = wg_d.rearrange("(kc kp) c -> kp kc c", kp=128)
                wp_v = wp_d.rearrange("(kc kp) c -> kp kc c", kp=128)
                for (tok0, ntok) in token_groups():
                    tiles = [(a, min(128, ntok - a)) for a in range(0, ntok, 128)]
                    DMA('pool', bufX16[:, :, :ntok], X1T_s[:, :, tok0:tok0 + ntok].rearrange("k p t -> p k t"), writes=['bufX16'])
                    with ExitStack() as st2:
                        nt = len(tiles)
                        gs = [sb(st2, f"gs{i}", [128, 2, 1024]) for i in range(nt)]
                        gE = [sb16(st2, f"gE{i}", [128, 2, 1024]) for i in range(nt)]
                        gth = [sb(st2, f"gth{i}", [128, 24]) for i in range(nt)]
                        UTs = [sb16(st2, f"UT{i}", [128, 16, 128]) for i in range(4)]; utr = Rot([0, 1, 2, 3])
                        gEf_t = sb(st2, "gEf", [128, D])
                        gEf = gEf_t[:, :].rearrange("p (x n) -> p x n", x=2)
                        vt16 = [[sb16(st2, f"vt16_{g}_{i}", [128, D]) for i in range(4)] for g in range(2)]
                        gaT = [sb16(st2, f"gaT{g}", [128, 4, 512]) for g in range(2)]
                        NW = 3
                        w1s = [sb16(st2, f"w1{i}", [128, 1024]) for i in range(NW)]
                        w2s = [sb16(st2, f"w2{i}", [128, 1024]) for i in range(NW)]
                        w3s = [sb16(st2, f"w3{i}", [128, 1024]) for i in range(NW)]
                        ntaus = [sb(st2, f"ntau{i}", [128, 8]) for i in range(NW)]
                        Gds = [sb(st2, f"Gd{i}", [128, 4, 128]) for i in range(2)]
                        coef16 = [sb16(st2, f"coef{i}", [128, 4, 128]) for i in range(2)]
                        for ti, (a, T) in enumerate(tiles):
                            A('pool', 'memset', [], [('acc', ti)], ap=acc[:T, ti, :], constant=0.0)
                            DMA('sp', gs[ti][:T, :, :], PG_s[tok0 + a:tok0 + a + T, :, :], writes=[('gs', ti)])
                            DMA('sp', gth[ti][:T, :], PGT_s[tok0 + a:tok0 + a + T, :], writes=[('gth', ti)])
                            for half in range(2):
                                A('dve', 'tensor_tensor', [('gs', ti), ('gth', ti)], [('vst', 0)], out=gEf[:T, half, :].rearrange("t (h n) -> t h n", n=128),
                                  in0=gs[ti][:T, half, :].rearrange("t (h n) -> t h n", n=128),
                                  in1=gth[ti][:T, 8 + 8 * half:16 + 8 * half].unsqueeze(2).to_broadcast([T, 8, 128]), op=ALU.subtract)
                            A('act', 'activation', [('vst', 0)], [('vst', 0)], out=gEf[:T, :, :], in_=gEf[:T, :, :], func=AF.Exp)
                            A('dve', 'tensor_copy', [('vst', 0)], [('gE', ti)], out=gE[ti][:T, 0, :], in_=gEf[:T, 0, :])
                            A('dve', 'tensor_scalar', [('vst', 0)], [('gE', ti)], out=gE[ti][:T, 1, :], in0=gEf[:T, 1, :], scalar1=0.5, scalar2=None, op0=ALU.mult)

                        def u_side(ag, j):
                            g = ag % 2
                            ab = ag * 4 + j
                            DMA('sp', vt16[g][j][:, :], V16_s[ab * 128:(ab + 1) * 128, :], writes=[('vt16', g, j)])
                            iut = utr.next(); UT = UTs[iut]
                            DMA('sp', UT[:, :, :].rearrange("p k e -> p (k e)"), UT16_s[ab, :, :], writes=[('UT', iut)])
                            pA, kA = ps()
                            for kc in range(16):
                                A('pe', 'matmul', [('UT', iut), 'bufX16'], [kA], out=pA[:, :ntok], lhsT=UT[:, kc, :], rhs=bufX16[:, kc, :ntok], start=(kc == 0), stop=(kc == 15))
                            A('act', 'activation', [kA], [('gaT', g, j)], out=gaT[g][:, j, :ntok], in_=pA[:, :ntok], func=AF.Gelu)

                        items = [(ag, ti, j) for ag in range(32) for ti in range(nt) for j in range(4)]
                        jsplit = [[j for j in range(4) if (j * nt) // 4 == ti] for ti in range(nt)]

                        def phaseA0(k):
                            ag, ti, j = items[k]; a, T = tiles[ti]; b = k % NW
                            s0v = gs[ti][:T, 0, :].rearrange("t (h n) -> t h n", n=128)
                            ab = ag * 4 + j
                            A('dve', 'scalar_tensor_tensor', [('gs', ti), ('gth', ti)], [('ntau', b)], out=ntaus[b][:T, :], in0=s0v[:, :, ab], scalar=DELTA,
                              in1=gth[ti][:T, 0:8], op0=ALU.add, op1=ALU.subtract)

                        def phaseA(k):
                            ag, ti, j = items[k]; a, T = tiles[ti]; b = k % NW
                            s1v = gs[ti][:T, 1, :].rearrange("t (h n) -> t h n", n=128)
                            w1v = w1s[b][:T, :].rearrange("t (h n) -> t h n", n=128)
                            for h in range(8):
                                A('act', 'activation', [('gs', ti), ('ntau', b)], [('w1', b)], out=w1v[:, h, :], in_=s1v[:, h, :], func=AF.Sign,
                                  bias=ntaus[b][:T, h:h + 1], scale=1.0)

                        def phaseBC(k):
                            ag, ti, j = items[k]; a, T = tiles[ti]; b = k % NW
                            ab = ag * 4 + j
                            e0v = gE[ti][:T, 0, :].rearrange("t (h n) -> t h n", n=128)
                            A('dve', 'scalar_tensor_tensor', [('w1', b), ('gE', ti)], [('w2', b)], out=w2s[b][:T, :], in0=w1s[b][:T, :], scalar=1.0, in1=gE[ti][:T, 1, :],
                              op0=ALU.add, op1=ALU.mult)
                            A('pool', 'tensor_tensor', [('w2', b), ('gE', ti)], [('w3', b)], out=w3s[b][:T, :].rearrange("t (h n) -> t h n", n=128),
                              in0=w2s[b][:T, :].rearrange("t (h n) -> t h n", n=128), in1=e0v[:, :, ab:ab + 1].to_broadcast([T, 8, 128]), op=ALU.mult)

                        def phaseD(k):
                            ag, ti, j = items[k]; a, T = tiles[ti]; b = k % NW
                            g = ag % 2
                            c = (ag * nt + ti) % 2
                            A('dve', 'tensor_reduce', [('w3', b)], [('Gd', c)], out=Gds[c][:T, j, :], in_=w3s[b][:T, :].rearrange("t (h n) -> t n h", n=128), axis=AX.X, op=ALU.add)
                            if j < 3:
                                return
                            pT_, kT_ = ps()
                            for jj in range(4):
                                A('pe', 'transpose', [('Gd', c), 'cst'], [kT_], out=pT_[:, jj * 128:jj * 128 + T], in_=Gds[c][:T, jj, :], identity=ident[:T, :T])
                            A('dve', 'tensor_tensor', [kT_] + [('gaT', g, jj) for jj in range(4)], [('coef', c)], out=coef16[c][:, :, :T],
                              in0=pT_[:, :].rearrange("p (j t) -> p j t", t=128)[:, :, :T], in1=gaT[g][:, :, a:a + T], op=ALU.mult)
                            for cbk in range(4):
                                po, ko = ps()
                                for jj in range(4):
                                    A('pe', 'matmul', [('coef', c), ('vt16', g, jj)], [ko], out=po[:T, :], lhsT=coef16[c][:, jj, :T],
                                      rhs=vt16[g][jj][:, cbk * 512:(cbk + 1) * 512], start=(jj == 0), stop=(jj == 3))
                                A('dve', 'tensor_tensor', [ko, ('acc', ti)], [('acc', ti)], out=acc[:T, ti, cbk * 512:(cbk + 1) * 512],
                                  in0=po[:T, :], in1=acc[:T, ti, cbk * 512:(cbk + 1) * 512], op=ALU.add)

                        for j in range(4):
                            u_side(0, j)
                        nitems = len(items)
                        LAGD = 3
                        phaseA0(0)
                        for k in range(nitems + LAGD):
                            if 0 <= k - LAGD < nitems:
                                ag, ti, j = items[k - LAGD]
                                if j == 0 and ag + 1 < 32:
                                    for jn in jsplit[ti]:
                                        u_side(ag + 1, jn)
                            if k + 1 < nitems:
                                phaseA0(k + 1)
                            if k < nitems:
                                phaseA(k)
                            if 0 <= k - 1 < nitems:
                                phaseBC(k - 1)
                            if 0 <= k - LAGD < nitems:
                                phaseD(k - LAGD)
                    P.barrier()
                    with ExitStack() as st3:
                        g2t = bcast_load(st3, "ln2g", ln2g_d, D)
                        b2t = bcast_load(st3, "ln2b", ln2b_d, D)
                        lnt = (sb(st3, "ln_stats", [128, 4, 6]), sb(st3, "ln_mv", [128, 2]), sb(st3, "ln_sd", [128, 2]))
                        x1t = sb(st3, "x1t5", [128, D]); r2 = sb(st3, "r2", [128, D])
                        bufX = sb(st3, "bufX5", [128, 16, 512])
                        wgs = [sb(st3, f"wg{i}", [128, 16, 256]) for i in range(2)]; wgr = Rot([0, 1])
                        wps = [sb(st3, f"wp{i}", [128, 2, 256]) for i in range(2)]
                        ptl = sb(st3, "ptl", [128, 256]); pT = sb(st3, "pT", [128, 2, 512])
                        gsb = [sb(st3, f"gsb{i}", [128, 256]) for i in range(2)]; yb = [sb(st3, f"yb{i}", [128, 256]) for i in range(2)]
                        gr5 = Rot([0, 1])
                        for ti, (a, T) in enumerate(tiles):
                            DMA('sp', x1t[:T, :], X1_s[tok0 + a:tok0 + a + T, :], writes=['x1t5'])
                            A('dve', 'scalar_tensor_tensor', ['x1t5', ('acc', ti)], ['r2'], out=r2[:T, :], in0=x1t[:T, :], scalar=ALPHA, in1=acc[:T, ti, :],
                              op0=ALU.mult, op1=ALU.add)
                            layer_norm(lnt, r2, 'r2', T, g2t, "ln2g", b2t, "ln2b", acc[:, ti, :], ('acc', ti))
                            transpose_to(acc[:, ti, :], ('acc', ti), T, bufX, 'bufX', a)
                            DMA('sp', ptl[:T, :], p_d[tok0 + a:tok0 + a + T, :], writes=['ptl'])
                            transpose_to(ptl, 'ptl', T, pT, 'pT', a, nkc=2)
                        for cb in range(8):
                            c0 = cb * 256
                            iw = wgr.next()
                            DMA('act', wgs[iw][:], wg_v[:, :, c0:c0 + 256], writes=[('wg', iw)])
                            DMA('act', wps[iw][:], wp_v[:, :, c0:c0 + 256], writes=[('wp', iw)])
                            for ti, (a, T) in enumerate(tiles):
                                pg_, kg_ = ps()
                                for kc in range(16):
                                    A('pe', 'matmul', ['bufX', ('wg', iw)], [kg_], out=pg_[:T, 0:256], lhsT=bufX[:, kc, a:a + T], rhs=wgs[iw][:, kc, :],
                                      start=(kc == 0), stop=(kc == 15))
                                ig_ = gr5.next()
                                A('act', 'activation', [kg_], [('gsb', ig_)], out=gsb[ig_][:T, :], in_=pg_[:T, 0:256], func=AF.Sigmoid)
                                pp_, kp_ = ps()
                                for kc in range(2):
                                    A('pe', 'matmul', ['pT', ('wp', iw)], [kp_], out=pp_[:T, 0:256], lhsT=pT[:, kc, a:a + T], rhs=wps[iw][:, kc, :],
                                      start=(kc == 0), stop=(kc == 1))
                                A('dve', 'tensor_tensor', [kp_, ('gsb', ig_)], [('yb', ig_)], out=yb[ig_][:T, :], in0=pp_[:T, 0:256], in1=gsb[ig_][:T, :], op=ALU.mult)
                                A('pool', 'tensor_tensor', [('yb', ig_), ('acc', ti)], [('yb', ig_)], out=yb[ig_][:T, :], in0=yb[ig_][:T, :], in1=acc[:T, ti, c0:c0 + 256],
                                  op=ALU.add)
                                DMA('sp', y_d[tok0 + a:tok0 + a + T, c0:c0 + 256], yb[ig_][:T, :], reads=[('yb', ig_)])
                    P.barrier()

        stage0()
        stage1()
        if debug_stage is None or debug_stage >= 2:
            stage2()
        if debug_stage is None or debug_stage >= 3:
            stage3()
        if debug_stage is None or debug_stage >= 4:
            stage45()
        P.emit(top)
    return nc


def make_consts():
    c = np.zeros((128, 640), np.float32)
    i = np.arange(128)
    c[:, 0:128] = np.eye(128, dtype=np.float32)
    c[:, 128:256] = (i[:, None] <= i[None, :]).astype(np.float32)
    c[:, 256:384] = np.where(i[None, :] > i[:, None], NEG, 0.0)
    c[:, 384:512] = np.where(i[None, :] >= i[:, None], NEG, 0.0)
    c[:, 512:640] = np.where(i[None, :] < i[:, None], NEG, 0.0)
    return c


_NC_CACHE = {}


def kernel(x_prompt, x_sample, state_gdn_conv, state_gdn_s, state_mlstm_c, state_mlstm_n, state_mlstm_m,
           p_prompt, p_sample, ln0_g, ln0_b, w_in, gdn_conv_w, gdn_a_log, gdn_dt_bias, gdn_norm_w,
           mlstm_b_i, mlstm_b_f, mlstm_norm_w, w_branch_a, w_branch_b, w_out, ln1_g, ln1_b,
           peer_wq, peer_keys, peer_u, peer_v, ln2_g, ln2_b, ple_proj, ple_gate):
    NCORES = 8
    f = np.float32
    x_prompt = np.asarray(x_prompt, f); x_sample = np.asarray(x_sample, f)
    B, LP, _ = x_prompt.shape
    BS, LS, _ = x_sample.shape
    NP = B // NCORES
    NS = BS // NCORES
    key = (NP, LP, NS, LS)
    if key not in _NC_CACHE:
        _NC_CACHE[key] = build(NP, LP, NS, LS)
    nc = _NC_CACHE[key]

    def c(a):
        return np.ascontiguousarray(np.asarray(a, f))

    shared = {
        "ln0_g": c(ln0_g), "ln0_b": c(ln0_b), "w_in": c(np.asarray(w_in)[0]), "gdn_conv_w": c(np.asarray(gdn_conv_w)[0]),
        "gdn_a_log": c(np.asarray(gdn_a_log)[0]), "gdn_dt_bias": c(np.asarray(gdn_dt_bias)[0]), "gdn_norm_w": c(np.asarray(gdn_norm_w)[0]),
        "mlstm_b_i": c(np.asarray(mlstm_b_i)[0]), "mlstm_b_f": c(np.asarray(mlstm_b_f)[0]), "mlstm_norm_w": c(np.asarray(mlstm_norm_w)[0]),
        "w_branch_a": c(np.asarray(w_branch_a)[0]), "w_branch_b": c(np.asarray(w_branch_b)[0]), "w_out": c(np.asarray(w_out)[0]),
        "ln1_g": c(np.asarray(ln1_g)[0]), "ln1_b": c(np.asarray(ln1_b)[0]), "peer_wq": c(np.asarray(peer_wq)[0]),
        "peer_keys": c(np.asarray(peer_keys)[0]).reshape(16, 128, 128), "peer_u": c(np.asarray(peer_u)[0]), "peer_v": c(np.asarray(peer_v)[0]),
        "ln2_g": c(np.asarray(ln2_g)[0]), "ln2_b": c(np.asarray(ln2_b)[0]), "ple_proj": c(np.asarray(ple_proj)[0]), "ple_gate": c(np.asarray(ple_gate)[0]),
        "consts": make_consts(),
    }
    pp = np.asarray(p_prompt, f)[0]; psm = np.asarray(p_sample, f)[0]
    sconv = np.asarray(state_gdn_conv, f)[0]; ssv = np.asarray(state_gdn_s, f)[0]
    scv = np.asarray(state_mlstm_c, f)[0]; snv = np.asarray(state_mlstm_n, f)[0]; smv = np.asarray(state_mlstm_m, f)[0]
    in_maps = []
    for ci in range(NCORES):
        pi = slice(ci * NP, (ci + 1) * NP); si = slice(ci * NS, (ci + 1) * NS)
        m = dict(shared)
        m["x"] = np.ascontiguousarray(np.concatenate([x_prompt[pi].reshape(NP * LP, D), x_sample[si].reshape(NS * LS, D)], 0))
        m["p"] = np.ascontiguousarray(np.concatenate([pp[pi].reshape(NP * LP, 256), psm[si].reshape(NS * LS, 256)], 0))
        m["sconv"] = c(sconv[si]); m["ss"] = c(ssv[si]); m["sc"] = c(scv[si]); m["sn"] = c(snv[si]); m["sm"] = c(smv[si])
        in_maps.append(m)
    res = run_bass_kernel_spmd(nc, in_maps, core_ids=list(range(NCORES)))
    rs = res.results
    y_p = np.empty((B, LP, D), f); y_s = np.empty((BS, LS, D), f)
    names = [("oconv", (3, 6144)), ("os", (16, 128, 128)), ("oc", (8, 128, 256)), ("on", (8, 128)), ("om", (8,))]
    pst = [np.empty((1, B) + shp, f) for _, shp in names]
    sst = [np.empty((1, BS) + shp, f) for _, shp in names]
    for ci in range(NCORES):
        r = rs[ci]
        y = np.asarray(r["y"])
        y_p[ci * NP:(ci + 1) * NP] = y[:NP * LP].reshape(NP, LP, D)
        y_s[ci * NS:(ci + 1) * NS] = y[NP * LP:].reshape(NS, LS, D)
        for k, (nm, shp) in enumerate(names):
            a = np.asarray(r[nm])
            pst[k][0, ci * NP:(ci + 1) * NP] = a[:NP]
            sst[k][0, ci * NS:(ci + 1) * NS] = a[NP:]
    return (y_p, y_s, *pst, *sst)
```
